# Optimizing a Trainium2 kernel written in Bass

```python
import jax, jax.numpy as jnp
from jax import lax
import numpy as np

D_MODEL = 1024
BATCH = 8
SEQ = 2048
DEPTH = 1
DEC_BATCH = 32
DEC_SEQ = 1
PAST_LEN = 16384
PAGE_SIZE = 128

N_META = 16
FFN_DIM = 2816
MLA_HEADS = 8
MLA_NOPE = 64
MLA_ROPE = 32
MLA_V = 64
Q_RANK = 384
KV_RANK = 256
MLA_SCALE = (MLA_NOPE + MLA_ROPE) ** -0.5
Q_BLOCK = 128
RET_HEADS = 4
RET_DK = 128
RET_DV = 256
RET_CHUNK = 128
ROPE_BASE = 10000.0
NORM_EPS = 1e-6
SPLITS = (Q_RANK, KV_RANK, MLA_ROPE, RET_HEADS * RET_DK, RET_HEADS * RET_DK,
          RET_HEADS * RET_DV, RET_HEADS * RET_DV, D_MODEL, D_MODEL)
IN_DIM = Q_RANK + KV_RANK + MLA_ROPE + 2 * RET_HEADS * RET_DK + 2 * RET_HEADS * RET_DV + 2 * D_MODEL

kernel_name = "mla_retention_gated_macaron_step"


def rms_norm(x, g):
    xf = x.astype(jnp.float32)
    y = xf * lax.rsqrt(jnp.mean(xf * xf, axis=-1, keepdims=True) + NORM_EPS)
    return (y * g.astype(jnp.float32)).astype(x.dtype)


def half_ffn(h, norm, w_gate, w_up, w_down):
    u = rms_norm(h, norm)
    return h + 0.5 * ((jax.nn.silu(u @ w_gate) * (u @ w_up)) @ w_down)


def rope(x, pos):
    half = x.shape[-1] // 2
    inv_freq = ROPE_BASE ** (-jnp.arange(half, dtype=jnp.float32) / half)
    ang = pos.astype(jnp.float32)[:, None] * inv_freq[None, :]
    ang = ang.reshape((ang.shape[0],) + (1,) * (x.ndim - 3) + (half,))
    cos, sin = jnp.cos(ang), jnp.sin(ang)
    xf = x.astype(jnp.float32)
    x1, x2 = xf[..., :half], xf[..., half:]
    return jnp.concatenate([x1 * cos - x2 * sin, x1 * sin + x2 * cos], axis=-1).astype(x.dtype)


def ret_log_gamma():
    return jnp.log1p(-jnp.exp2(-5.0 - jnp.arange(RET_HEADS, dtype=jnp.float32)))


def mixer_inputs(u, pos, W):
    B, T, _ = u.shape
    z = u @ W["w_in"]
    c_q, c_kv, k_r, rq, rk, rv, rg, ga, gb = jnp.split(z, np.cumsum(SPLITS)[:-1].tolist(), axis=-1)
    c_q = rms_norm(c_q, W["q_norm"])
    q = (c_q @ W["w_uq"]).reshape(B, T, MLA_HEADS, MLA_NOPE + MLA_ROPE)
    q_lat = jnp.einsum("bthn,rhn->bthr", q[..., :MLA_NOPE], W["w_uk"])
    q_rope = rope(q[..., MLA_NOPE:], pos)
    c_kv = rms_norm(c_kv, W["kv_norm"])
    k_r = rope(k_r, pos)
    rq = rope(rq.reshape(B, T, RET_HEADS, RET_DK), pos)
    rk = rope(rk.reshape(B, T, RET_HEADS, RET_DK), pos) * (RET_DK ** -0.5)
    rv = rv.reshape(B, T, RET_HEADS, RET_DV)
    return q_lat, q_rope, c_kv, k_r, rq, rk, rv, rg, ga, gb


def latent_attend(q_lat, q_rope, ckv, krope, q_pos, k_pos):
    s = (jnp.einsum("bqhr,bkr->bhqk", q_lat, ckv, preferred_element_type=jnp.float32)
         + jnp.einsum("bqhp,bkp->bhqk", q_rope, krope, preferred_element_type=jnp.float32))
    s = jnp.where(k_pos[None, :] <= q_pos[:, None], s * MLA_SCALE, -jnp.inf)
    p = jax.nn.softmax(s, axis=-1)
    return jnp.einsum("bhqk,bkr->bqhr", p.astype(ckv.dtype), ckv)


def prompt_latent_attention(q_lat, q_rope, c_kv, k_r):
    B, T = q_lat.shape[:2]
    n_blk = -(-T // Q_BLOCK)
    pad = n_blk * Q_BLOCK - T

    def blocks(a):
        a = jnp.pad(a, ((0, 0), (0, pad)) + ((0, 0),) * (a.ndim - 2))
        return jnp.moveaxis(a.reshape((B, n_blk, Q_BLOCK) + a.shape[2:]), 1, 0)

    q_pos = jnp.arange(n_blk * Q_BLOCK).reshape(n_blk, Q_BLOCK)
    k_pos = jnp.arange(T)
    o = lax.map(lambda blk: latent_attend(blk[0], blk[1], c_kv, k_r, blk[2], k_pos),
                (blocks(q_lat), blocks(q_rope), q_pos))
    o = jnp.moveaxis(o, 0, 1).reshape((B, n_blk * Q_BLOCK) + o.shape[3:])
    return o[:, :T]


def ret_chunk(q, k, v, S, log_gamma):
    C = q.shape[1]
    q, k, v, S = (a.astype(jnp.float32) for a in (q, k, v, S))
    idx = jnp.arange(C, dtype=jnp.float32)
    diff = idx[:, None] - idx[None, :]
    decay = jnp.where(diff >= 0, jnp.exp(log_gamma[:, None, None] * jnp.maximum(diff, 0.0)), 0.0)
    scores = jnp.einsum("bihd,bjhd->bhij", q, k) * decay
    o = jnp.einsum("bhij,bjhe->bihe", scores, v)
    q_decay = jnp.exp(log_gamma[None, :] * (idx[:, None] + 1.0))
    o = o + jnp.einsum("bihd,bhde->bihe", q, S) * q_decay[None, :, :, None]
    k_decay = jnp.exp(log_gamma[None, :] * (C - 1.0 - idx[:, None]))
    S_new = (jnp.exp(log_gamma * C)[None, :, None, None] * S
             + jnp.einsum("bjhd,bjhe->bhde", k * k_decay[None, :, :, None], v))
    return o, S_new


def prompt_retention(rq, rk, rv):
    B, T, H = rq.shape[:3]
    lg = ret_log_gamma()
    S0 = jnp.zeros((B, H, RET_DK, RET_DV), jnp.float32)
    o_meta, S = ret_chunk(rq[:, :N_META], rk[:, :N_META], rv[:, :N_META], S0, lg)
    n_c = (T - N_META) // RET_CHUNK

    def chunks(a):
        return jnp.moveaxis(a[:, N_META:].reshape((B, n_c, RET_CHUNK) + a.shape[2:]), 1, 0)

    def step(S, qkv):
        o, S = ret_chunk(qkv[0], qkv[1], qkv[2], S, lg)
        return S, o

    S, o = lax.scan(step, S, (chunks(rq), chunks(rk), chunks(rv)))
    o = jnp.moveaxis(o, 0, 1).reshape(B, T - N_META, H, RET_DV)
    return jnp.concatenate([o_meta, o], axis=1), S


def mixer_output(h, o_lat, o_ret, rg, ga, gb, W):
    B, T = h.shape[:2]
    a = jnp.einsum("bthr,rhv->bthv", o_lat, W["w_uv"]).reshape(B, T, MLA_HEADS * MLA_V) @ W["w_mla_o"]
    of = o_ret.astype(jnp.float32)
    mu = jnp.mean(of, axis=-1, keepdims=True)
    var = jnp.mean(jnp.square(of - mu), axis=-1, keepdims=True)
    on = ((of - mu) * lax.rsqrt(var + NORM_EPS)).reshape(B, T, RET_HEADS * RET_DV) * W["ret_gn"].astype(jnp.float32)
    r = (jax.nn.silu(rg) * on.astype(h.dtype)) @ W["w_ret_o"]
    m = jax.nn.sigmoid(ga) * a + jax.nn.sigmoid(gb) * r
    return h + m @ W["w_out"]


def prompt_layer(x, W):
    T = x.shape[1]
    pos = jnp.arange(T)
    h = half_ffn(x, W["ffn1_norm"], W["ffn1_gate"], W["ffn1_up"], W["ffn1_down"])
    u = rms_norm(h, W["mix_norm"])
    q_lat, q_rope, c_kv, k_r, rq, rk, rv, rg, ga, gb = mixer_inputs(u, pos, W)
    o_lat = prompt_latent_attention(q_lat, q_rope, c_kv, k_r)
    o_ret, s_ret = prompt_retention(rq, rk, rv)
    h = mixer_output(h, o_lat, o_ret, rg, ga, gb, W)
    h = half_ffn(h, W["ffn2_norm"], W["ffn2_gate"], W["ffn2_up"], W["ffn2_down"])
    return h, c_kv, k_r, s_ret


def sample_layer(x, ckv_pool, kr_pool, s_ret, page_table, W):
    B, S, _ = x.shape
    past_len = page_table.shape[1] * PAGE_SIZE
    pos = past_len + jnp.arange(S)
    h = half_ffn(x, W["ffn1_norm"], W["ffn1_gate"], W["ffn1_up"], W["ffn1_down"])
    u = rms_norm(h, W["mix_norm"])
    q_lat, q_rope, c_kv, k_r, rq, rk, rv, rg, ga, gb = mixer_inputs(u, pos, W)
    past_ckv = ckv_pool[page_table].reshape(B, past_len, KV_RANK)
    past_kr = kr_pool[page_table].reshape(B, past_len, MLA_ROPE)
    keys_ckv = jnp.concatenate([past_ckv, c_kv.astype(past_ckv.dtype)], axis=1)
    keys_kr = jnp.concatenate([past_kr, k_r.astype(past_kr.dtype)], axis=1)
    o_lat = latent_attend(q_lat, q_rope, keys_ckv, keys_kr, pos, jnp.arange(past_len + S))
    o_ret, s_new = ret_chunk(rq, rk, rv, s_ret, ret_log_gamma())
    h = mixer_output(h, o_lat, o_ret, rg, ga, gb, W)
    h = half_ffn(h, W["ffn2_norm"], W["ffn2_gate"], W["ffn2_up"], W["ffn2_down"])
    return h, c_kv, k_r, s_new


def setup_inputs(seed: int = 0) -> dict:
    key = jax.random.key(seed)
    ks = iter(jax.random.split(key, 32))
    f32 = jnp.float32
    L = DEPTH

    def nrm(shape, scale=1.0):
        return jax.random.normal(next(ks), shape, f32) * scale

    def gain(shape):
        return 1.0 + nrm(shape, 0.02)

    n_pages = PAST_LEN // PAGE_SIZE
    n_used = DEC_BATCH * n_pages
    n_pool = n_used + n_used // 4
    x_prompt = nrm((BATCH, SEQ, D_MODEL))
    x_sample = nrm((DEC_BATCH, DEC_SEQ, D_MODEL))
    cache_ckv = nrm((L, n_pool, PAGE_SIZE, KV_RANK))
    cache_krope = nrm((L, n_pool, PAGE_SIZE, MLA_ROPE))
    state_ret = nrm((L, DEC_BATCH, RET_HEADS, RET_DK, RET_DV), 0.5)
    page_table = jax.random.permutation(next(ks), n_pool)[:n_used].reshape(DEC_BATCH, n_pages).astype(jnp.int32)
    return {
        "x_prompt": x_prompt,
        "x_sample": x_sample,
        "cache_ckv": cache_ckv,
        "cache_krope": cache_krope,
        "state_ret": state_ret,
        "page_table": page_table,
        "meta_tokens": nrm((N_META, D_MODEL)),
        "ffn1_norm": gain((L, D_MODEL)),
        "ffn1_gate": nrm((L, D_MODEL, FFN_DIM), D_MODEL ** -0.5),
        "ffn1_up": nrm((L, D_MODEL, FFN_DIM), D_MODEL ** -0.5),
        "ffn1_down": nrm((L, FFN_DIM, D_MODEL), FFN_DIM ** -0.5),
        "mix_norm": gain((L, D_MODEL)),
        "w_in": nrm((L, D_MODEL, IN_DIM), D_MODEL ** -0.5),
        "q_norm": gain((L, Q_RANK)),
        "kv_norm": gain((L, KV_RANK)),
        "w_uq": nrm((L, Q_RANK, MLA_HEADS * (MLA_NOPE + MLA_ROPE)), Q_RANK ** -0.5),
        "w_uk": nrm((L, KV_RANK, MLA_HEADS, MLA_NOPE), KV_RANK ** -0.5),
        "w_uv": nrm((L, KV_RANK, MLA_HEADS, MLA_V), KV_RANK ** -0.5),
        "w_mla_o": nrm((L, MLA_HEADS * MLA_V, D_MODEL), (MLA_HEADS * MLA_V) ** -0.5),
        "ret_gn": gain((L, RET_HEADS * RET_DV)),
        "w_ret_o": nrm((L, RET_HEADS * RET_DV, D_MODEL), (RET_HEADS * RET_DV) ** -0.5),
        "w_out": nrm((L, D_MODEL, D_MODEL), D_MODEL ** -0.5),
        "ffn2_norm": gain((L, D_MODEL)),
        "ffn2_gate": nrm((L, D_MODEL, FFN_DIM), D_MODEL ** -0.5),
        "ffn2_up": nrm((L, D_MODEL, FFN_DIM), D_MODEL ** -0.5),
        "ffn2_down": nrm((L, FFN_DIM, D_MODEL), FFN_DIM ** -0.5),
        "final_norm": gain((D_MODEL,)),
    }


def reference(x_prompt, x_sample, cache_ckv, cache_krope, state_ret, page_table, meta_tokens,
              ffn1_norm, ffn1_gate, ffn1_up, ffn1_down, mix_norm, w_in, q_norm, kv_norm,
              w_uq, w_uk, w_uv, w_mla_o, ret_gn, w_ret_o, w_out,
              ffn2_norm, ffn2_gate, ffn2_up, ffn2_down, final_norm):
    B = x_prompt.shape[0]
    meta = jnp.broadcast_to(meta_tokens[None].astype(x_prompt.dtype), (B, N_META, D_MODEL))
    hp = jnp.concatenate([meta, x_prompt], axis=1)
    hs = x_sample
    ckv_p, kr_p, sr_p, ckv_s, kr_s, sr_s = [], [], [], [], [], []
    for l in range(DEPTH):
        W = dict(ffn1_norm=ffn1_norm[l], ffn1_gate=ffn1_gate[l], ffn1_up=ffn1_up[l], ffn1_down=ffn1_down[l],
                 mix_norm=mix_norm[l], w_in=w_in[l], q_norm=q_norm[l], kv_norm=kv_norm[l],
                 w_uq=w_uq[l], w_uk=w_uk[l], w_uv=w_uv[l], w_mla_o=w_mla_o[l],
                 ret_gn=ret_gn[l], w_ret_o=w_ret_o[l], w_out=w_out[l],
                 ffn2_norm=ffn2_norm[l], ffn2_gate=ffn2_gate[l], ffn2_up=ffn2_up[l], ffn2_down=ffn2_down[l])
        hp, c1, k1, s1 = prompt_layer(hp, W)
        hs, c2, k2, s2 = sample_layer(hs, cache_ckv[l], cache_krope[l], state_ret[l], page_table, W)
        ckv_p.append(c1); kr_p.append(k1); sr_p.append(s1)
        ckv_s.append(c2); kr_s.append(k2); sr_s.append(s2)
    y_prompt = rms_norm(hp[:, N_META:], final_norm)
    y_sample = rms_norm(hs, final_norm)
    return (y_prompt, y_sample, jnp.stack(ckv_p), jnp.stack(kr_p), jnp.stack(sr_p),
            jnp.stack(ckv_s), jnp.stack(kr_s), jnp.stack(sr_s))
```

```python
import contextlib
import numpy as np
import concourse.bass as bass
import concourse.mybir as mybir
from concourse.bass_utils import run_bass_kernel_spmd

F32 = mybir.dt.float32
BF16 = mybir.dt.bfloat16
I32 = mybir.dt.int32
ALU = mybir.AluOpType
AF = mybir.ActivationFunctionType
AX = mybir.AxisListType

P = 128
D = 1024
KD = 8
FF = 2816
KF = 22
NMETA = 16
NSMP = 4
NSIDE = NMETA + NSMP
CH = 512
SEQ = 2048
NCORES = 8
QR = 384
KVR = 256
ROPE = 32
NOPE = 64
HM = 8
HD = NOPE + ROPE
VD = 64
RH = 4
RDK = 128
RDV = 256
PAGE = 128
NPAGES = 128
PAST = NPAGES * PAGE
NPOOL = 5120
EPS = 1e-6
MLA_SCALE = float(HD) ** -0.5
IN_DIM = QR + KVR + ROPE + 2 * RH * RDK + 2 * RH * RDV + 2 * D
OFF_CQ = 0
OFF_CKV = QR
OFF_KR = QR + KVR
OFF_RQ = OFF_KR + ROPE
OFF_RK = OFF_RQ + RH * RDK
OFF_RV = OFF_RK + RH * RDK
OFF_RG = OFF_RV + RH * RDV
OFF_GA = OFF_RG + RH * RDV
OFF_GB = OFF_GA + D


class Buf:
    __slots__ = ("name", "w", "r", "excl")

    def __init__(self, name, excl=False):
        self.name = name
        self.w = None
        self.r = []
        self.excl = excl


class Op:
    __slots__ = ("eng", "fn", "deps", "dma_sem", "dma_val", "signal", "count", "idx", "seq", "todo", "known")

    def __init__(self, eng, fn):
        self.eng = eng
        self.fn = fn
        self.deps = []
        self.dma_sem = None
        self.dma_val = 0
        self.signal = False
        self.count = 0
        self.idx = 0


COMPUTE = ("pe", "act", "dve", "pool")


class Sched:
    def __init__(self):
        self.ops = {e: [] for e in ("pe", "act", "dve", "pool", "sp")}
        self.dma_counts = {}
        self.out_dmas = []
        self.all_ops = []

    def add(self, eng, fn, reads=(), writes=(), dma_sem=None, nodep_same_pe=True):
        op = Op(eng, fn)
        deps = {}
        for b in reads:
            if b.w is not None:
                deps[id(b.w)] = b.w
            if b.excl:
                for r in b.r:
                    if r.eng != eng:
                        deps[id(r)] = r
        for b in writes:
            if b.w is not None:
                deps[id(b.w)] = b.w
            for r in b.r:
                deps[id(r)] = r
        for d in deps.values():
            if d is op:
                continue
            if d.eng == "pe" and eng == "pe":
                continue
            op.deps.append(d)
        for b in reads:
            b.r.append(op)
        for b in writes:
            b.w = op
            b.r = []
        if dma_sem is not None:
            op.dma_sem = dma_sem
            self.dma_counts[dma_sem] = self.dma_counts.get(dma_sem, 0) + 16
            op.dma_val = self.dma_counts[dma_sem]
        op.idx = len(self.ops[eng])
        self.ops[eng].append(op)
        op.seq = len(self.all_ops)
        self.all_ops.append(op)
        return op

    def finalize(self):
        for e, lst in self.ops.items():
            for op in lst:
                for d in op.deps:
                    if d.dma_sem is None:
                        d.signal = True
        for e in COMPUTE:
            c = 0
            for op in self.ops[e]:
                if op.signal:
                    c += 1
                    op.count = c
        waited = {e: {} for e in self.ops}
        for op in self.all_ops:
            w = waited[op.eng]
            need = {}
            for d in op.deps:
                key = ("d", d.dma_sem) if d.dma_sem is not None else ("e", d.eng)
                val = d.dma_val if d.dma_sem is not None else d.count
                if key not in need or need[key][0] < val:
                    need[key] = (val, d)
            todo = []
            for key, (val, d) in sorted(need.items(), key=lambda kv: -kv[1][1].seq):
                if w.get(key, 0) < val:
                    todo.append((key, val))
                    w[key] = val
                for k2, v2 in d.known.items():
                    if w.get(k2, 0) < v2:
                        w[k2] = v2
            op.todo = todo
            op.known = dict(w)

    def emit(self, nc, sems, dma_sems, final_waits, pre_sp=None):
        engobj = {"pe": "tensor", "act": "scalar", "dve": "vector", "pool": "gpsimd", "sp": "sync"}

        def run(ename, eng):
            if ename == "sp" and pre_sp is not None:
                pre_sp(eng)
            waited = {}
            for op in self.ops[ename]:
                todo = [(dma_sems[key[1]] if key[0] == "d" else sems[key[1]], val) for key, val in op.todo]
                for sem, val in todo[:-1]:
                    eng.wait_ge(sem, val)
                ins = op.fn(eng)
                if todo:
                    ins._wait_ge(todo[-1][0], todo[-1][1])
                if op.dma_sem is not None:
                    ins.then_inc(dma_sems[op.dma_sem], 16)
                elif op.signal:
                    ins.then_inc(sems[ename], 1)
            if ename == "sp":
                for name, val in final_waits:
                    eng.wait_ge(dma_sems[name], val)

        with nc.Block() as block:
            @block.sync
            def _(e):
                run("sp", e)

            @block.tensor
            def _(e):
                run("pe", e)

            @block.scalar
            def _(e):
                run("act", e)

            @block.vector
            def _(e):
                run("dve", e)

            @block.gpsimd
            def _(e):
                run("pool", e)


class Builder:
    def __init__(self, cfg):
        self.cfg = cfg
        self.nch = cfg.get("nch", 4)
        self.npages = cfg.get("npages", NPAGES)
        self.stages = cfg.get("stages", "full")
        self.nc = bass.Bass("TRN2", target_bir_lowering=False)
        self.S = Sched()
        self.es = contextlib.ExitStack()
        self.dma_sem_names = []
        self.n_act_dve = 0
        self.regs = []
        self.sp_regs = []
        self.npool = cfg.get('npool', NPOOL)

    def sb(self, name, shape, dt):
        t = self.es.enter_context(self.nc.sbuf_tensor(name, list(shape), dt))
        return t

    def dram_in(self, name, shape, dt=F32):
        return self.nc.dram_tensor(name, list(shape), dt, kind="ExternalInput").ap()

    def dram_out(self, name, shape, dt=F32):
        return self.nc.dram_tensor(name, list(shape), dt, kind="ExternalOutput").ap()

    def dsem(self, name):
        if name not in self.dma_sem_names:
            self.dma_sem_names.append(name)
        return name

    def pe(self, fn, reads, writes):
        return self.S.add("pe", fn, reads, writes)

    def act(self, fn, reads, writes):
        return self.S.add("act", fn, reads, writes)

    def dve(self, fn, reads, writes):
        return self.S.add("dve", fn, reads, writes)

    def pool(self, fn, reads, writes):
        return self.S.add("dve", fn, reads, writes)

    def anyv(self, fn, reads, writes):
        self.n_act_dve += 1
        if self.n_act_dve % 3 == 0:
            return self.pool(fn, reads, writes)
        return self.dve(fn, reads, writes)

    def dma(self, out, in_, reads, writes, sem, **kw):
        self.dsem(sem)
        return self.S.add("sp", lambda e: e.dma_start(out=out, in_=in_, **kw), reads, writes, dma_sem=sem)

    def activation(self, out, in_, func, reads, writes, eng="act", **kw):
        return self.S.add(eng, lambda e: e.activation(out=out, in_=in_, func=func, **kw), reads, writes)

    def tt(self, eng, out, in0, in1, op, reads, writes):
        eng = "dve" if eng == "pool" else eng
        return self.S.add(eng, lambda e: e.tensor_tensor(out=out, in0=in0, in1=in1, op=op), reads, writes)

    def stt(self, eng, out, in0, scalar, in1, op0, op1, reads, writes):
        eng = "dve" if eng == "pool" else eng
        return self.S.add(eng, lambda e: e.scalar_tensor_tensor(out=out, in0=in0, scalar=scalar, in1=in1,
                                                                 op0=op0, op1=op1), reads, writes)

    def ts(self, eng, out, in0, s1, s2, op0, op1, reads, writes):
        eng = "dve" if eng == "pool" else eng
        if op1 is None:
            return self.S.add(eng, lambda e: e.tensor_scalar(out=out, in0=in0, scalar1=s1, scalar2=None, op0=op0),
                              reads, writes)
        return self.S.add(eng, lambda e: e.tensor_scalar(out=out, in0=in0, scalar1=s1, scalar2=s2, op0=op0, op1=op1),
                          reads, writes)

    def copy(self, eng, out, in_, reads, writes):
        eng = "dve" if eng == "pool" else eng
        if eng == "act":
            return self.S.add(eng, lambda e: e.activation(out=out, in_=in_, func=AF.Copy), reads, writes)
        return self.S.add(eng, lambda e: e.tensor_copy(out=out, in_=in_), reads, writes)

    def recip(self, out, in_, reads, writes):
        return self.S.add("dve", lambda e: e.reciprocal(out=out, in_=in_), reads, writes)

    def memset(self, eng, ap, val, reads, writes):
        eng = "dve" if eng == "pool" else eng
        return self.S.add(eng, lambda e: e.memset(ap, val), reads, writes)

    def mm(self, out, lhsT, rhs, start, stop, reads, writes):
        return self.pe(lambda e: e.matmul(out, lhsT, rhs, start=start, stop=stop), reads, writes)

    def tr(self, out, in_, ident, reads, writes):
        return self.pe(lambda e: e.transpose(out, in_, ident), reads, writes)

    def dbg(self, name, ap, shape, buf, dt=F32):
        if not self.cfg.get("debug"):
            return
        o = self.dram_out("dbg_" + name, shape, dt)
        self.dma(o, ap, [buf], [], self.dsem("o_dbg_" + name))

    def init_psum(self):
        self.banks = []
        self.bank_bufs = []
        for i in range(8):
            t = self.es.enter_context(self.nc.psum_tensor(f"ps{i}", [P, 512], F32))
            self.banks.append(t)
            self.bank_bufs.append(Buf(f"ps{i}", excl=True))
        self.bank_i = 0
        self.bank2_i = 0

    NROT = 4

    def bank(self):
        i = self.bank_i
        self.bank_i = (i + 1) % self.NROT
        return self.banks[i], self.bank_bufs[i]

    def bank2(self):
        i = 4 + self.bank2_i
        self.bank2_i = (self.bank2_i + 1) % 4
        return self.banks[i], self.bank_bufs[i]

    def rbank(self, i):
        return self.banks[i], self.bank_bufs[i]

    WSLOT = 2048

    def init_wring(self):
        self.nwb = 6
        self.wb = [self.sb(f"wb{i}", [P, self.WSLOT], BF16) for i in range(self.nwb)]
        self.wb_b = [Buf(f"wb{i}") for i in range(self.nwb)]
        self.wb_i = 0

    def load_w(self, src, a, b, rows=P):
        assert a * b <= self.WSLOT
        r = self.wb_i
        self.wb_i = (r + 1) % self.nwb
        wv = self.wb[r][0:rows, 0:a * b].rearrange("p (a b) -> p a b", a=a)
        self.dsem(f"wb{r}")
        self.S.add("pool", lambda e: e.dma_start(out=wv, in_=src), [], [self.wb_b[r]], dma_sem=f"wb{r}")
        return wv, self.wb_b[r]

    def build(self):
        nc = self.nc
        nch = self.nch
        npages = self.npages
        T = NMETA + nch * CH
        NTT = 1 + nch * 4
        B = Buf

        x = self.dram_in("x", [nch * CH, D])
        xs = self.dram_in("xs", [NSMP, D])
        meta = self.dram_in("meta", [NMETA, D])
        W = {}
        for l in ("1", "2"):
            W["n" + l] = self.dram_in(f"ffn{l}_norm", [D])
            W["g" + l] = self.dram_in(f"ffn{l}_gate", [D, FF])
            W["u" + l] = self.dram_in(f"ffn{l}_up", [D, FF])
            W["d" + l] = self.dram_in(f"ffn{l}_down", [FF, D])
        W["fn"] = self.dram_in("final_norm", [D])
        W["mixn"] = self.dram_in("mix_norm", [D])
        w_in = self.dram_in("w_in", [D, IN_DIM])
        q_norm = self.dram_in("q_norm", [QR])
        kv_norm = self.dram_in("kv_norm", [KVR])
        w_uq = self.dram_in("w_uq", [QR, HM * HD])
        w_uk = self.dram_in("w_uk", [KVR, HM * NOPE])
        w_uv = self.dram_in("w_uv", [KVR, HM * VD])
        w_mla_o = self.dram_in("w_mla_o", [HM * VD, D])
        ret_gn = self.dram_in("ret_gn", [RH * RDV])
        w_ret_o = self.dram_in("w_ret_o", [RH * RDV, D])
        w_out = self.dram_in("w_out", [D, D])
        cache_ckv = self.dram_in("cache_ckv", [self.npool, PAGE, KVR])
        cache_kr = self.dram_in("cache_krope", [self.npool, PAGE, ROPE])
        state = self.dram_in("state", [NSMP, RH, RDK, RDV])
        ptab = self.dram_in("ptab", [NSMP, npages], I32)
        c_ident = self.dram_in("c_ident", [P, P])
        c_cosR = self.dram_in("c_cosR", [NTT, P, 64])
        c_sinR = self.dram_in("c_sinR", [NTT, P, 64])
        c_cosM = self.dram_in("c_cosM", [NTT, P, 16])
        c_sinM = self.dram_in("c_sinM", [NTT, P, 16])
        c_maskT = self.dram_in("c_maskT", [P, RH, P])
        c_qdec = self.dram_in("c_qdec", [P, RH, P])
        c_kdec = self.dram_in("c_kdec", [P, 8])
        c_tri = self.dram_in("c_tri", [P, P])
        c_onehot = self.dram_in("c_onehot", [P, NSMP])
        c_bmask = self.dram_in("c_bmask", [32, 512])
        c_sel2 = self.dram_in("c_sel2", [32, NSIDE])
        c_smask = self.dram_in("c_smask", [8, NSMP * NSIDE])
        y = self.dram_out("y", [nch * CH, D])
        ys = self.dram_out("ys", [NSMP, D])
        ckv_p = self.dram_out("ckv_p", [T, KVR])
        kr_p = self.dram_out("kr_p", [T, ROPE])
        S_p = self.dram_out("S_p", [RH, RDK, RDV])
        ckv_s = self.dram_out("ckv_s", [NSMP, KVR])
        kr_s = self.dram_out("kr_s", [NSMP, ROPE])
        S_s = self.dram_out("S_s", [NSMP, RH, RDK, RDV])

        self.init_psum()
        self.init_wring()

        def bfv(ps):
            return ps[:, :].bitcast(BF16)

        hT = self.sb("hT", [P, KD, CH], F32); hT_b = B("hT")
        uT = self.sb("uT", [P, KD, CH], BF16); uT_b = B("uT")
        rstd = self.sb("rstd", [P, CH], F32); rstd_b = B("rstd")
        ckvT_all = self.sb("ckvT_all", [P, 2, T], BF16); ckvT_b = B("ckvT")
        krT_all = self.sb("krT_all", [96, T], BF16); krT_b = B("krT")
        Vt = self.sb("Vt", [P, NTT, HM, VD + 1], BF16); V_b = B("V")
        kTh = self.sb("kTh", [96, T], BF16); kTh_b = B("kTh")
        S = self.sb("S", [P, RH, RDV], F32); S_bf = self.sb("S_bf", [P, RH, RDV], BF16); S_b = B("S"); Sbf_b = B("Sbf")
        w_uk_sb = self.sb("w_uk_sb", [P, 2, HM * NOPE], BF16)
        w_uv_sb = self.sb("w_uv_sb", [P, 2, HM * VD], BF16)
        w_ukT = self.sb("w_ukT", [NOPE, HM, KVR], BF16)
        wres_b = B("wres")
        cosR = self.sb("cosR", [P, 4, 64], F32); sinR = self.sb("sinR", [P, 4, 64], F32)
        cosM = self.sb("cosM", [P, 4, 16], F32); sinM = self.sb("sinM", [P, 4, 16], F32)
        rope_b = B("rope")
        maskT = self.sb("maskT", [P, RH, P], F32)
        qdec = self.sb("qdec", [P, RH, P], F32)
        kdec = self.sb("kdec", [P, 8], F32)
        tri_f = self.sb("tri_f", [P, P], F32)
        tri = self.sb("tri", [P, P], BF16)
        onehot = self.sb("onehot", [P, NSMP], F32)
        bmask = self.sb("bmask", [32, 512], F32)
        sel2_f = self.sb("sel2_f", [32, NSIDE], F32)
        sel2 = self.sb("sel2", [32, NSIDE], BF16)
        smask = self.sb("smask", [8, NSMP * NSIDE], F32)
        gains = self.sb("gains", [P, 4, KD], F32)
        qn_bc = self.sb("qn_bc", [P, QR], F32)
        kvn_bc = self.sb("kvn_bc", [P, KVR], F32)
        gn_bc = self.sb("gn_bc", [P, RH * RDV], F32)
        ident_f = self.sb("ident_f", [P, P], F32)
        ident_b = self.sb("ident_b", [P, P], BF16)
        onesD = self.sb("onesD", [P, P], BF16)
        ones_c = self.sb("ones_c", [P, 2], BF16)
        qropeT = self.sb("qropeT", [ROPE, HM, NSIDE], BF16)
        epsb = self.sb("epsb", [P, 1], F32)
        const_b = B("const")
        stat = self.sb("stat", [P, 32], F32)
        stat_b = B("stat")
        stat2 = self.sb("stat2", [P, 32], F32)
        stat2_b = B("stat2")
        stat3 = self.sb("stat3", [P, 4], F32)
        stat3_b = B("stat3")
        NLS = npages // 4 + 1
        lsum = self.sb("lsum", [P, NSMP * NLS], F32)
        lsum_b = [B(f"lsum{i}") for i in range(NSMP)]
        pidx_sb = self.sb("pidx_sb", [P, NSMP * npages], I32)
        iota_sb = self.sb("iota_sb", [P, 1], F32)
        ptab_b = B("ptab")
        c_iota = self.dram_in("c_iota", [P, 1])
        ckv_rows = cache_ckv.rearrange("n (q u) c -> (n q) (u c)", u=4)
        kr_rows = cache_kr.rearrange("n (q u) c -> (n q) (u c)", u=4)

        USZ = 87 * 1024
        U = self.sb("U", [P, USZ // 2], BF16)

        def uv(off_bytes, nelem, dt, shape_str=None, **kw):
            esz = 2 if dt == BF16 else 4
            assert off_bytes % 4 == 0
            v = U[:, off_bytes // 2: off_bytes // 2 + nelem * esz // 2]
            if dt != BF16:
                v = v.bitcast(dt)
            if shape_str:
                v = v.rearrange(shape_str, **kw)
            return v

        hid = uv(0, KF * CH, BF16, "p (k n) -> p k n", k=KF); hid_b = B("hid")
        sq = uv(22528, KD * CH, BF16, "p (k n) -> p k n", k=KD); sq_kb = [B(f"sq{k_}") for k_ in range(KD)]
        sgt = [uv(30720 + i * 1024, CH, BF16) for i in range(2)]; sgt_b = [B("sgt0"), B("sgt1")]
        xin = [uv(i * 4096, D, F32) for i in range(4)]
        yout = [uv(22528 + i * 4096, D, F32) for i in range(2)]
        ZS = 3072

        def zf(slot, o, n):
            return uv(slot * ZS * 2 + o * 2, n, BF16)
        rq_tm = [zf(s_, 0, 512) for s_ in range(4)]
        rk_tm = [zf(s_, 512, 512) for s_ in range(4)]
        rv_tm = [zf(s_, 1024, 1024) for s_ in range(4)]
        rgg = [zf(s_, 2048, 1024) for s_ in range(4)]
        ga_s = [zf(s_, 0, 1024) for s_ in range(4)]
        gb_s = rv_tm
        m_tm = rgg
        zq_b = B("zq"); zv_b = B("zv"); zg_b = B("zg")
        o = 4 * ZS * 2
        qT = uv(o, HM * CH, BF16, "p (h n) -> p h n", h=HM); qT_b = B("qT"); o += 8192
        mT = qT; mT_b = qT_b
        cqnT = uv(o, 3 * CH, BF16, "p (k n) -> p k n", k=3); cqnT_b = B("cqnT"); o += 3072
        aT = uv(o, 4 * CH, BF16, "p (k n) -> p k n", k=4); aT_b = B("aT"); o += 4096
        rinT = uv(o, KD * CH, BF16, "p (k n) -> p k n", k=KD); rinT_b = B("rinT"); o += 8192
        q_tm = uv(o, 4 * HM * HD, BF16, "p (s h c) -> p s h c", s=4, h=HM); q_tm_b = B("q_tm"); o += 6144
        a_tm = uv(o - 6144, 4 * 512, BF16, "p (s c) -> p s c", s=4); a_tm_b = q_tm_b
        cqn_tm = uv(o, QR, BF16); cqn_tm_b = B("cqn_tm"); o += 768
        ckv_tm = [uv(o + i * 1024, KVR, F32) for i in range(2)]; ckv_tm_b = [B("ckv_tm0"), B("ckv_tm1")]; o += 2048
        kr_tm = [uv(o + i * 128, ROPE, F32) for i in range(2)]; kr_tm_b = [B("kr_tm0"), B("kr_tm1")]; o += 256
        krpad = uv(o, 96, BF16); krpad_b = B("krpad"); o += 192
        ropeA = uv(o, 256, F32, "p (h c) -> p h c", h=4); o += 1024
        ropeB = uv(o, 256, F32, "p (h c) -> p h c", h=4); o += 1024
        rope_t_b = B("ropeT")
        NPT = 4
        pTs = [uv(o + i * 1024, CH, BF16) for i in range(NPT)]; pT_b = [B(f"pT{i}") for i in range(NPT)]; o += NPT * 1024
        r_in = uv(o, 1024, BF16); r_in_b = B("r_in"); o += 2048
        ma = uv(o, 512, F32); ma_b = B("ma"); o += 2048
        cens = [uv(o + i * 4096, 1024, F32, "p (h e) -> p h e", h=4) for i in range(2)]; cens_b = [B("cen0"), B("cen1")]; o += 8192
        junk = ma; junk_b = ma_b
        rt = {}
        for nm_ in ("qT_sb", "qdT", "kT_sb", "sTm", "kd"):
            rt[nm_] = uv(o, RH * P, BF16); o += RH * P * 2
        ret_b = {k_: B(k_) for k_ in rt}
        Sp = uv(o, RDV, F32); o += 1024
        Sn = uv(o, RDV, F32); o += 1024
        Snb = uv(o, RDV, BF16); o += 512
        vm = uv(o, RDV, BF16); o += 512
        sret_b = B("sret")
        qTm = uv(o, RH * NSMP * NSIDE, BF16, "p (h s j) -> p h s j", h=RH, s=NSMP); qTm_b = B("qTm"); o += 640
        ckvTs = uv(o, 2 * NSIDE, BF16, "p (k j) -> p k j", k=2); o += 80
        krTs = uv(o, NSIDE + 4, BF16); o += 48
        qlatT = uv(o, 2 * HM * NSIDE, BF16, "p (k h j) -> p k h j", k=2, h=HM); o += 640
        ckvs_bf = uv(o, 258, BF16); o += 516 + 4
        o = (o + 3) // 4 * 4
        side_b = B("sidemisc")
        olT = uv(o, 2 * 32, BF16, "p (k j) -> p k j", k=2); o += 128
        am = uv(o, 512, BF16); o += 1024
        p20 = uv(o, NSIDE, F32); o += 80
        p20b = uv(o, NSIDE, BF16); o += 40
        pT20 = uv(o, 8, BF16); o += 16
        ol_sb = uv(o, KVR, BF16); o += 512
        smp_b = B("smpmisc")
        assert o <= USZ, o
        self.u_used = o
        so = ZS * 2
        NATW = 290
        NNAT = 4
        natc = [uv(so + i * 2304, 4 * KVR, BF16, "p (g c) -> p g c", g=4) for i in range(NNAT)]
        natk = [uv(so + i * 2304 + 2048, 4 * ROPE, BF16, "p (g c) -> p g c", g=4) for i in range(NNAT)]
        so += NNAT * 2304
        NSET = 2
        ckvT_sb, krT_sb, p_sb, pT_sb, sblk_b = [], [], [], [], []
        for i in range(NSET):
            ckvT_sb.append(uv(so, 1024, BF16)); so += 2048
            krT_sb.append(uv(so, 512, BF16)); so += 1024
            p_sb.append(uv(so, 512, BF16)); so += 1024
            pT_sb.append(uv(so, 32, BF16)); so += 64
            sblk_b.append({k_: B(k_ + str(i)) for k_ in ("ckvT_sb", "krT_sb", "p_sb", "pT_sb")})
        assert so <= 4 * ZS * 2, so
        natb_b = [B(f"natb{i}") for i in range(NNAT)]

        def cload(dst, src, **kw):
            self.dma(dst, src, [], [const_b], "const", **kw)
        cload(ident_f[:, :], c_ident)
        nblk_ = npages // 4
        for j in range(4):
            self.dma(pidx_sb[32 * j:32 * (j + 1), 0:NSMP * nblk_],
                     ptab.rearrange("s (b j) -> j (s b)", j=4)[j].partition_broadcast(32), [], [ptab_b], "ptab",
                     allow_slow_non_contiguous=True)
        self.dma(iota_sb[:, :], c_iota, [], [ptab_b], "ptab")
        idf = uv(0, NSMP * nblk_, F32)
        self.copy("dve", idf, pidx_sb[:, 0:NSMP * nblk_], [ptab_b], [ptab_b, hid_b])
        self.ts("dve", idf, idf, 32.0, iota_sb[:, 0:1], ALU.mult, ALU.add, [ptab_b], [ptab_b, hid_b])
        self.copy("dve", pidx_sb[:, 0:NSMP * nblk_], idf, [ptab_b], [ptab_b, hid_b])
        cload(maskT[:, :, :], c_maskT)
        cload(qdec[:, :, :], c_qdec)
        cload(kdec[:, :], c_kdec)
        cload(tri_f[:, :], c_tri)
        cload(onehot[:, :], c_onehot)
        cload(bmask[:, :], c_bmask)
        cload(sel2_f[:, :], c_sel2)
        cload(smask[:, :], c_smask)
        cload(qn_bc[:, :], q_norm.partition_broadcast(P))
        cload(kvn_bc[:, :], kv_norm.partition_broadcast(P))
        cload(gn_bc[:, :], ret_gn.partition_broadcast(P))
        for gi, g in enumerate([W["n1"], W["mixn"], W["n2"], W["fn"]]):
            cload(gains[:, gi, :], g.rearrange("(k p) -> p k", p=P), allow_slow_non_contiguous=True)
        self.copy("dve", ident_b[:, :], ident_f[:, :], [const_b], [const_b])
        self.copy("dve", tri[:, :], tri_f[:, :], [const_b], [const_b])
        self.copy("dve", sel2[:, :], sel2_f[:, :], [const_b], [const_b])
        self.memset("pool", onesD[:, :], 1.0 / D, [], [const_b])
        self.memset("pool", epsb[:, :], EPS, [], [const_b])
        self.memset("pool", ones_c[:, :], 1.0, [], [const_b])
        self.memset("pool", Vt[:, :, :, VD:VD + 1], 1.0, [], [V_b])
        self.memset("pool", stat[:, :], 0.0, [], [stat_b])
        wv_, wvb_ = self.load_w(w_uk.rearrange("(k p) c -> p k c", p=P), 2, HM * NOPE)
        self.copy("pool", w_uk_sb[:, :, :], wv_, [wvb_], [wres_b])
        wv_, wvb_ = self.load_w(w_uv.rearrange("(k p) c -> p k c", p=P), 2, HM * VD)
        self.copy("pool", w_uv_sb[:, :, :], wv_, [wvb_], [wres_b])
        for hh in range(2):
            ps, psb = self.bank()
            v = bfv(ps)
            for h4 in range(4):
                h = hh * 4 + h4
                for rk in range(2):
                    self.tr(v[0:NOPE, (h4 * 2 + rk) * P:(h4 * 2 + rk + 1) * P], w_uk_sb[:, rk, h * NOPE:(h + 1) * NOPE],
                            ident_b[:, :], [wres_b, const_b], [psb])
            self.copy("dve", w_ukT[:, hh * 4:(hh + 1) * 4, :],
                      v[0:NOPE, 0:1024].rearrange("p (h c) -> p h c", h=4), [psb], [wres_b])

        tgl = [0]

        def evac_copy(out, in_, reads, writes):
            tgl[0] += 1
            return self.copy("act" if tgl[0] % 2 else "dve", out, in_, reads, writes)

        class _Stop(Exception):
            pass
        stop_at = self.cfg.get("stop_at")

        def chk(name):
            if stop_at == name:
                raise _Stop()

        class Tile:
            pass

        def mk_pass(kind, c):
            p_ = Tile()
            p_.kind = kind
            p_.c = c
            p_.tiles = []
            if kind == "side":
                p_.n = NSIDE
                t = Tile(); t.slot = 0; t.gt = 0; t.rows = NSIDE; t.c0 = 0; t.g0 = 0; t.gn = NMETA
                p_.tiles.append(t)
            else:
                p_.n = CH
                for tt in range(4):
                    t = Tile(); t.slot = tt; t.gt = 1 + c * 4 + tt; t.rows = P; t.c0 = tt * P
                    t.g0 = NMETA + (c * 4 + tt) * P; t.gn = P
                    p_.tiles.append(t)
            return p_

        def rms_stats(n):
            ps, psb = self.bank()
            for k in range(KD):
                if k % 2 == 0:
                    self.activation(sq[:, k, 0:n], hT[:, k, 0:n], AF.Square, [hT_b], [sq_kb[k]])
                else:
                    self.tt("dve", sq[:, k, 0:n], hT[:, k, 0:n], hT[:, k, 0:n], ALU.mult, [hT_b], [sq_kb[k]])
                self.mm(ps[:, 0:n], onesD[:, :], sq[:, k, 0:n], k == 0, k == KD - 1, [sq_kb[k], const_b], [psb])
            self.activation(rstd[:, 0:n], ps[:, 0:n], AF.Sqrt, [psb, const_b], [rstd_b], bias=epsb[:, 0:1], scale=1.0)
            self.recip(rstd[:, 0:n], rstd[:, 0:n], [rstd_b], [rstd_b])

        def rmsnorm_fm(n, gi):
            rms_stats(n)
            for k in range(KD):
                self.stt("dve", uT[:, k, 0:n], hT[:, k, 0:n], gains[:, gi, k:k + 1], rstd[:, 0:n], ALU.mult, ALU.mult,
                         [hT_b, rstd_b, const_b], [uT_b])

        def ffn(n, gi, Wg, Wu, Wd):
            rmsnorm_fm(n, gi)
            FG = 2
            for jg in range(KF // FG):
                wg, wgb = self.load_w(Wg[:, jg * FG * P:(jg + 1) * FG * P].rearrange("(k p) f -> p k f", p=P), KD, FG * P)
                wu, wub = self.load_w(Wu[:, jg * FG * P:(jg + 1) * FG * P].rearrange("(k p) f -> p k f", p=P), KD, FG * P)
                for jj in range(FG):
                    j = jg * FG + jj
                    psg, psgb = self.bank()
                    psu, psub = self.bank()
                    for k in range(KD):
                        self.mm(psg[:, 0:n], wg[:, k, jj * P:(jj + 1) * P], uT[:, k, 0:n], k == 0, k == KD - 1, [wgb, uT_b], [psgb])
                    for k in range(KD):
                        self.mm(psu[:, 0:n], wu[:, k, jj * P:(jj + 1) * P], uT[:, k, 0:n], k == 0, k == KD - 1, [wub, uT_b], [psub])
                    si = j % 2
                    self.activation(sgt[si][:, 0:n], psg[:, 0:n], AF.Silu, [psgb], [sgt_b[si]])
                    self.tt("dve", hid[:, j, 0:n], sgt[si][:, 0:n], psu[:, 0:n], ALU.mult, [sgt_b[si], psub], [hid_b])
            dbanks = [self.rbank(i) for i in range(KD)]
            for kg in range(KF // 2):
                w, wbuf = self.load_w(Wd[kg * 2 * P:(kg + 1) * 2 * P, :].rearrange("(k p) d -> p k d", p=P), 2, D)
                for kk in range(2):
                    k = kg * 2 + kk
                    for i in range(KD):
                        self.mm(dbanks[i][0][:, 0:n], w[:, kk, i * P:(i + 1) * P], hid[:, k, 0:n], k == 0, k == KF - 1,
                                [wbuf, hid_b], [dbanks[i][1]])
            for i in range(KD):
                self.stt("dve", hT[:, i, 0:n], dbanks[i][0][:, 0:n], 0.5, hT[:, i, 0:n], ALU.mult, ALU.add, [dbanks[i][1], hT_b], [hT_b])

        def load_x_dma(pas):
            if pas.kind == "side":
                self.dma(xin[0][0:NMETA, :], meta, [], [hid_b], "xin")
                self.dma(xin[0][NMETA:NSIDE, :], xs, [], [hid_b], "xin")
            else:
                for tt in range(4):
                    r0 = (pas.c * 4 + tt) * P
                    self.dma(xin[tt][:, :], x[r0:r0 + P, :], [], [hid_b], "xin")

        def load_x(pas):
            if pas.kind == "side":
                for k in range(KD):
                    ps, psb = self.bank()
                    self.tr(ps[:, 0:NSIDE], xin[0][0:NSIDE, k * P:(k + 1) * P], ident_f[0:NSIDE, 0:NSIDE], [hid_b, const_b], [psb])
                    evac_copy(hT[:, k, 0:NSIDE], ps[:, 0:NSIDE], [psb], [hT_b])
            else:
                for k in range(KD):
                    ps, psb = self.bank()
                    for tt in range(4):
                        self.tr(ps[:, tt * P:(tt + 1) * P], xin[tt][:, k * P:(k + 1) * P], ident_f[:, :], [hid_b, const_b], [psb])
                    evac_copy(hT[:, k, :], ps[:, :], [psb], [hT_b])

        def final_norm_out(pas):
            n = pas.n
            rms_stats(n)
            for k in range(KD):
                self.stt("dve", hT[:, k, 0:n], hT[:, k, 0:n], gains[:, 3, k:k + 1], rstd[:, 0:n], ALU.mult, ALU.mult,
                         [hT_b, rstd_b, const_b], [hT_b])
            if pas.kind == "main":
                for tt in range(4):
                    yo = yout[tt % 2]
                    for kk in range(2):
                        ps, psb = self.bank()
                        for k4 in range(4):
                            k = kk * 4 + k4
                            self.tr(ps[:, k4 * P:(k4 + 1) * P], hT[:, k, tt * P:(tt + 1) * P], ident_f[:, :], [hT_b, const_b], [psb])
                        evac_copy(yo[:, kk * 512:(kk + 1) * 512], ps[:, :], [psb], sq_kb[(tt % 2) * 4:(tt % 2) * 4 + 4])
                    r0 = (pas.c * 4 + tt) * P
                    self.dma(y[r0:r0 + P, :], yo[:, :], sq_kb[(tt % 2) * 4:(tt % 2) * 4 + 4], [], f"o_y{tt % 2}")
            else:
                yo = yout[0]
                for kk in range(2):
                    ps, psb = self.bank()
                    for k4 in range(4):
                        k = kk * 4 + k4
                        self.mm(ps[0:NSIDE, k4 * P:(k4 + 1) * P], hT[:, k, 0:NSIDE], ident_f[:, :], True, True, [hT_b, const_b], [psb])
                    evac_copy(yo[0:NSIDE, kk * 512:(kk + 1) * 512], ps[0:NSIDE, :], [psb], sq_kb[0:4])
                self.dma(ys, yo[NMETA:NSIDE, :], sq_kb[0:4], [], "o_y0")

        hi_tgl = [0]

        def proj_tm(pas, srcT, srcb, KK, Wsrc, col0, ncols, evac, allow_hi=False):
            klen = min(KK, self.WSLOT // ncols)
            use_hi = False
            if allow_hi:
                hi_tgl[0] += 1
                use_hi = hi_tgl[0] % 2 == 1
            if use_hi:
                banks = [self.rbank(4 + i) for i in range(len(pas.tiles))]
            else:
                banks = [self.bank() for _ in pas.tiles]
            k0 = 0
            while k0 < KK:
                kl = min(klen, KK - k0)
                w, wbuf = self.load_w(Wsrc[k0 * P:(k0 + kl) * P, col0:col0 + ncols].rearrange("(k p) c -> p k c", p=P), kl, ncols)
                for ti, t in enumerate(pas.tiles):
                    ps, psb = banks[ti]
                    for kk in range(kl):
                        k = k0 + kk
                        self.mm(ps[0:t.rows, 0:ncols], srcT[:, k, t.c0:t.c0 + t.rows], w[:, kk, :], k == 0, k == KK - 1,
                                [srcb, wbuf], [psb])
                k0 += kl
            for ti, t in enumerate(pas.tiles):
                evac(t, banks[ti][0], banks[ti][1])

        def tm_rms(ps, psb, r, c0, ncols, gbc, out, outb):
            self.memset("pool", stat[0:r, 16:17], 0.0, [], [stat_b])
            self.activation(junk[0:r, 0:ncols], ps[0:r, c0:c0 + ncols], AF.Square, [psb, stat_b], [junk_b, stat_b],
                            accum_out=stat[0:r, 16:17])
            self.activation(stat[0:r, 17:18], stat[0:r, 16:17], AF.Sqrt, [stat_b, const_b], [stat_b],
                            bias=epsb[0:r, 0:1], scale=1.0 / ncols)
            self.recip(stat[0:r, 17:18], stat[0:r, 17:18], [stat_b], [stat_b])
            self.stt("dve", out, ps[0:r, c0:c0 + ncols], stat[0:r, 17:18], gbc[0:r, :], ALU.mult, ALU.mult,
                     [psb, stat_b, const_b], [outb])

        def rope_tm(x1, x2, cos, sin, o1, o2, r, nh, w, sc, reads, wb):
            A = ropeA[0:r, 0:nh, 0:w]
            Bt = ropeB[0:r, 0:nh, 0:w]
            rd = list(reads) + [rope_b]
            self.stt("dve", A, x1, sc, cos, ALU.mult, ALU.mult, rd, [rope_t_b])
            self.stt("dve", Bt, x2, sc, sin, ALU.mult, ALU.mult, rd, [rope_t_b])
            self.tt("dve", o1, A, Bt, ALU.subtract, [rope_t_b], [wb])
            self.stt("dve", A, x1, sc, sin, ALU.mult, ALU.mult, rd + [wb], [rope_t_b])
            self.stt("dve", Bt, x2, sc, cos, ALU.mult, ALU.mult, rd, [rope_t_b])
            self.tt("dve", o2, A, Bt, ALU.add, [rope_t_b], [wb])

        def bc(ap, r, nh, w):
            return ap.unsqueeze(1).broadcast_to([r, nh, w])

        def mixer_proj(pas):
            n = pas.n
            rmsnorm_fm(n, 1)
            g0t = pas.tiles[0].gt
            nt = len(pas.tiles)
            for dst, src in ((cosR, c_cosR), (sinR, c_sinR), (cosM, c_cosM), (sinM, c_sinM)):
                self.dma(dst[:, 0:nt, :], src[g0t:g0t + nt].rearrange("t p c -> p t c"), [], [rope_b], "rope")

            def ev_cq(t, ps, psb):
                r = t.rows
                tm_rms(ps, psb, r, 0, QR, qn_bc, cqn_tm[0:r, :], cqn_tm_b)
                pt, ptb = self.bank2()
                v = bfv(pt)
                for kq in range(3):
                    self.tr(v[:, kq * P:kq * P + r], cqn_tm[0:r, kq * P:(kq + 1) * P], ident_b[0:r, 0:r], [cqn_tm_b, const_b], [ptb])
                self.copy("act", cqnT[:, :, t.c0:t.c0 + r], v[:, 0:3 * P].rearrange("p (k c) -> p k c", k=3)[:, :, 0:r], [ptb], [cqnT_b])
            chk("p:a")
            proj_tm(pas, uT, uT_b, KD, w_in, OFF_CQ, QR, ev_cq)
            chk("p:cq")

            for half in range(2):
                def ev_q(t, ps, psb, half=half):
                    r = t.rows
                    psv = ps[0:r, 0:4 * HD].rearrange("p (h c) -> p h c", h=4)
                    dst = q_tm[0:r, t.slot, half * 4:(half + 1) * 4, :]
                    sk = self.cfg.get("dbgskip", "")
                    if "nope" not in sk:
                        self.copy("act", dst[:, :, 0:NOPE], psv[:, :, 0:NOPE], [psb], [q_tm_b])
                    if "rope" not in sk:
                        rope_tm(psv[:, :, 64:80], psv[:, :, 80:96], bc(cosM[0:r, t.slot, :], r, 4, 16), bc(sinM[0:r, t.slot, :], r, 4, 16),
                                dst[:, :, 64:80], dst[:, :, 80:96], r, 4, 16, 1.0, [psb], q_tm_b)
                proj_tm(pas, cqnT, cqnT_b, 3, w_uq, half * 4 * HD, 4 * HD, ev_q)
            chk("p:q")
            for h in range(HM):
                pt, ptb = self.bank()
                v = bfv(pt)
                for t in pas.tiles:
                    self.tr(v[0:HD, t.c0:t.c0 + t.rows], q_tm[0:t.rows, t.slot, h, :], ident_b[0:t.rows, 0:t.rows], [q_tm_b, const_b], [ptb])
                evac_copy(qT[0:HD, h, 0:n], v[0:HD, 0:n], [ptb], [qT_b])

            chk("p:qT")

            def ev_ckvkr(t, ps, psb):
                r = t.rows
                i2 = t.gt % 2
                ck = ckv_tm[i2]; ckb = ckv_tm_b[i2]
                tm_rms(ps, psb, r, 0, KVR, kvn_bc, ck[0:r, :], ckb)
                if pas.kind == "main":
                    self.dma(ckv_p[t.g0:t.g0 + P, :], ck[:, :], [ckb], [], f"o_ckv{i2}")
                else:
                    self.dma(ckv_p[0:NMETA, :], ck[0:NMETA, :], [ckb], [], f"o_ckv{i2}")
                    self.dma(ckv_s, ck[NMETA:NSIDE, :], [ckb], [], f"o_ckv{i2}")
                pt, ptb = self.bank2()
                for rk in range(2):
                    self.tr(pt[:, rk * P:rk * P + r], ck[0:r, rk * P:(rk + 1) * P], ident_f[0:r, 0:r], [ckb, const_b], [ptb])
                self.copy("act", ckvT_all[:, :, t.g0:t.g0 + t.gn], pt[:, 0:2 * P].rearrange("p (k c) -> p k c", k=2)[:, :, 0:t.gn],
                          [ptb], [ckvT_b])
                if pas.kind == "side":
                    self.copy("act", ckvTs[:, :, :], pt[:, 0:2 * P].rearrange("p (k c) -> p k c", k=2)[:, :, 0:NSIDE], [ptb], [side_b])
                    self.memset("pool", ckvs_bf[0:NSIDE, 0:1], 1.0, [], [side_b])
                    self.memset("pool", ckvs_bf[0:NSIDE, 1:2], 0.0, [], [side_b])
                    self.copy("dve", ckvs_bf[0:NSIDE, 2:258], ck[0:NSIDE, :], [ckb], [side_b])
                kr = kr_tm[i2]; krb = kr_tm_b[i2]
                A = ropeA[0:r, 0, 0:16]; Bt = ropeB[0:r, 0, 0:16]
                c_, s_ = cosM[0:r, t.slot, :], sinM[0:r, t.slot, :]
                x1, x2 = ps[0:r, 256:272], ps[0:r, 272:288]
                self.tt("dve", A, x1, c_, ALU.mult, [psb, rope_b], [rope_t_b])
                self.tt("dve", Bt, x2, s_, ALU.mult, [psb, rope_b], [rope_t_b])
                self.tt("dve", kr[0:r, 0:16], A, Bt, ALU.subtract, [rope_t_b], [krb])
                self.tt("dve", A, x1, s_, ALU.mult, [psb, rope_b, krb], [rope_t_b])
                self.tt("dve", Bt, x2, c_, ALU.mult, [psb, rope_b], [rope_t_b])
                self.tt("dve", kr[0:r, 16:32], A, Bt, ALU.add, [rope_t_b], [krb])
                if pas.kind == "main":
                    self.dma(kr_p[t.g0:t.g0 + P, :], kr[:, :], [krb], [], f"o_kr{i2}")
                else:
                    self.dma(kr_p[0:NMETA, :], kr[0:NMETA, :], [krb], [], f"o_kr{i2}")
                    self.dma(kr_s, kr[NMETA:NSIDE, :], [krb], [], f"o_kr{i2}")
                self.copy("dve", krpad[0:r, 64:96], kr[0:r, :], [krb], [krpad_b])
                pt2, pt2b = self.bank2()
                v2 = bfv(pt2)
                self.tr(v2[0:96, 0:r], krpad[0:r, 0:96], ident_b[0:r, 0:r], [krpad_b, const_b], [pt2b])
                self.copy("act", krT_all[64:96, t.g0:t.g0 + t.gn], v2[64:96, 0:t.gn], [pt2b], [krT_b])
                if pas.kind == "side":
                    self.copy("act", krTs[64:96, 0:NSIDE], v2[64:96, 0:NSIDE], [pt2b], [side_b])
                pv, pvb = self.bank2()
                for rk in range(2):
                    self.mm(pv[0:t.gn, 0:512], ckvT_all[:, rk, t.g0:t.g0 + t.gn], w_uv_sb[:, rk, :], rk == 0, rk == 1,
                            [ckvT_b, wres_b], [pvb])
                self.copy("dve", Vt[0:t.gn, t.gt, :, 0:VD], pv[0:t.gn, 0:512].rearrange("p (h v) -> p h v", h=HM), [pvb], [V_b])
            self.memset("pool", krpad[:, 0:64], 0.0, [], [krpad_b])
            proj_tm(pas, uT, uT_b, KD, w_in, OFF_CKV, KVR + ROPE, ev_ckvkr)

            chk("p:ckv")

            def ev_rqk(dst_list, sc):
                def ev(t, ps, psb):
                    r = t.rows
                    psv = ps[0:r, 0:512].rearrange("p (h c) -> p h c", h=RH)
                    dst = dst_list[t.slot][0:r, :].rearrange("p (h c) -> p h c", h=RH)
                    rope_tm(psv[:, :, 0:64], psv[:, :, 64:128], bc(cosR[0:r, t.slot, :], r, RH, 64), bc(sinR[0:r, t.slot, :], r, RH, 64),
                            dst[:, :, 0:64], dst[:, :, 64:128], r, RH, 64, sc, [psb], zq_b)
                return ev
            proj_tm(pas, uT, uT_b, KD, w_in, OFF_RQ, 512, ev_rqk(rq_tm, 1.0), allow_hi=True)
            proj_tm(pas, uT, uT_b, KD, w_in, OFF_RK, 512, ev_rqk(rk_tm, float(RDK) ** -0.5), allow_hi=True)
            chk("p:rqk")
            for half in range(2):
                def ev_rv(t, ps, psb, half=half):
                    evac_copy(rv_tm[t.slot][0:t.rows, half * 512:(half + 1) * 512], ps[0:t.rows, 0:512], [psb], [zv_b])
                proj_tm(pas, uT, uT_b, KD, w_in, OFF_RV + half * 512, 512, ev_rv, allow_hi=True)
            for half in range(2):
                def ev_rg(t, ps, psb, half=half):
                    r = t.rows
                    self.activation(ma[0:r, :], ps[0:r, 0:512], AF.Silu, [psb], [ma_b])
                    self.tt("dve", rgg[t.slot][0:r, half * 512:(half + 1) * 512], ma[0:r, :], gn_bc[0:r, half * 512:(half + 1) * 512],
                            ALU.mult, [ma_b, const_b], [zg_b])
                proj_tm(pas, uT, uT_b, KD, w_in, OFF_RG + half * 512, 512, ev_rg, allow_hi=True)

        def mla_main(pas):
            c = pas.c
            Tk = NMETA + CH * (c + 1)
            nkt = 1 + 4 * (c + 1)
            for h in range(HM):
                for g0c in range(0, Tk, 512):
                    ncol = min(512, Tk - g0c)
                    ps, psb = self.bank()
                    for rk in range(2):
                        self.mm(ps[0:NOPE, 0:ncol], w_uk_sb[:, rk, h * NOPE:(h + 1) * NOPE], ckvT_all[:, rk, g0c:g0c + ncol],
                                rk == 0, rk == 1, [wres_b, ckvT_b], [psb])
                    evac_copy(kTh[0:NOPE, g0c:g0c + ncol], ps[0:NOPE, 0:ncol], [psb], [kTh_b])
                self.copy("act", kTh[64:96, 0:Tk], krT_all[64:96, 0:Tk], [krT_b], [kTh_b])
                yield
                pos_ = [self.rbank(4 + tq) for tq in range(4)]

                def geom(kt):
                    if kt == 0:
                        kc0, nk, lt = 0, NMETA, -1
                    else:
                        kc0, nk, lt = NMETA + (kt - 1) * P, P, kt - 1 - 4 * c
                    tq0 = max(0, lt)
                    return kc0, nk, lt, tq0

                def qk(kt):
                    kc0, nk, lt, tq0 = geom(kt)
                    q0 = tq0 * P
                    N = CH - q0
                    ps, psb = self.bank()
                    self.mm(ps[0:nk, 0:N], kTh[0:HD, kc0:kc0 + nk], qT[0:HD, h, q0:CH], True, True, [kTh_b, qT_b], [psb])
                    pi = kt % NPT
                    pT = pTs[pi]
                    self.activation(pT[0:nk, 0:N], ps[0:nk, 0:N], AF.Exp, [psb], [pT_b[pi]], scale=MLA_SCALE)
                    if lt >= 0:
                        self.tt("dve", pT[:, 0:P], pT[:, 0:P], tri[:, :], ALU.mult, [pT_b[pi], const_b], [pT_b[pi]])

                def pv(kt):
                    kc0, nk, lt, tq0 = geom(kt)
                    pi = kt % NPT
                    pT = pTs[pi]
                    for tq in range(tq0, 4):
                        self.mm(pos_[tq][0][:, 0:VD + 1], pT[0:nk, (tq - tq0) * P:(tq - tq0 + 1) * P], Vt[0:nk, kt, h, :],
                                kt == 0, kt == 1 + 4 * c + tq, [pT_b[pi], V_b], [pos_[tq][1]])
                LAG = 2
                for kt in range(min(LAG, nkt)):
                    qk(kt)
                for kt in range(nkt):
                    if kt + LAG < nkt:
                        qk(kt + LAG)
                    pv(kt)
                    yield
                for tq in range(4):
                    self.recip(stat3[:, tq:tq + 1], pos_[tq][0][:, VD:VD + 1], [pos_[tq][1]], [stat3_b])
                    self.ts("dve", a_tm[:, tq, h * VD:(h + 1) * VD], pos_[tq][0][:, 0:VD], stat3[:, tq:tq + 1], None, ALU.mult, None,
                            [pos_[tq][1], stat3_b], [a_tm_b])
                yield

        def a_transpose(pas):
            for t in pas.tiles:
                r = t.rows
                pt, ptb = self.bank()
                v = bfv(pt)
                for k4 in range(4):
                    self.tr(v[:, k4 * P:k4 * P + r], a_tm[0:r, t.slot, k4 * P:(k4 + 1) * P], ident_b[0:r, 0:r], [a_tm_b, const_b], [ptb])
                evac_copy(aT[:, :, t.c0:t.c0 + r], v[:, 0:512].rearrange("p (k c) -> p k c", k=4)[:, :, 0:r], [ptb], [aT_b])

        lg = [float(np.log1p(-2.0 ** (-5.0 - h))) for h in range(RH)]

        def groupnorm_gate(t, ci):
            r = t.rows
            cen, cen_b = cens[ci], cens_b[ci]
            self.S.add("dve", lambda e: e.reduce_sum(out=stat[0:r, 0:4], in_=cen[0:r, :, :], axis=AX.X), [cen_b], [stat_b])
            self.ts("dve", stat[0:r, 4:8], stat[0:r, 0:4], -1.0 / RDV, None, ALU.mult, None, [stat_b], [stat_b])
            self.memset("pool", stat[0:r, 8:12], 0.0, [stat_b], [stat_b])
            for h in range(RH):
                self.activation(cen[0:r, h, :], cen[0:r, h, :], AF.Identity, [stat_b, cen_b], [cen_b], bias=stat[0:r, 4 + h:5 + h], scale=1.0)
            for h in range(RH):
                self.activation(junk[0:r, 0:RDV], cen[0:r, h, :], AF.Square, [cen_b, stat_b], [junk_b, stat_b],
                                accum_out=stat[0:r, 8 + h:9 + h])
            self.activation(stat[0:r, 12:16], stat[0:r, 8:12], AF.Sqrt, [stat_b, const_b], [stat_b], bias=epsb[0:r, 0:1], scale=1.0 / RDV)
            self.recip(stat[0:r, 12:16], stat[0:r, 12:16], [stat_b], [stat_b])
            for h in range(RH):
                self.stt("dve", r_in[0:r, h * RDV:(h + 1) * RDV], cen[0:r, h, :], stat[0:r, 12 + h:13 + h],
                         rgg[t.slot][0:r, h * RDV:(h + 1) * RDV], ALU.mult, ALU.mult, [cen_b, stat_b, zg_b], [r_in_b])
            pt, ptb = self.bank()
            v = bfv(pt)
            for k8 in range(KD):
                self.tr(v[:, k8 * P:k8 * P + r], r_in[0:r, k8 * P:(k8 + 1) * P], ident_b[0:r, 0:r], [r_in_b, const_b], [ptb])
            evac_copy(rinT[:, :, t.c0:t.c0 + r], v[:, 0:1024].rearrange("p (k c) -> p k c", k=KD)[:, :, 0:r], [ptb], [rinT_b])

        def retention_main(pas):
            W4 = RH * P
            for t in pas.tiles:
                sl = t.slot
                ci = t.slot % 2
                pt, ptb = self.bank()
                v = bfv(pt)
                for h in range(RH):
                    self.tr(v[:, h * P:(h + 1) * P], rq_tm[sl][:, h * P:(h + 1) * P], ident_b[:, :], [zq_b, const_b], [ptb])
                for h in range(RH):
                    self.tr(v[:, W4 + h * P:W4 + (h + 1) * P], rk_tm[sl][:, h * P:(h + 1) * P], ident_b[:, :], [zq_b, const_b], [ptb])
                self.copy("act", rt["qT_sb"][:, :], v[:, 0:W4], [ptb], [ret_b["qT_sb"]])
                self.tt("dve", rt["qdT"][:, :], v[:, 0:W4], qdec[:, :, :].rearrange("p h i -> p (h i)"), ALU.mult, [ptb, const_b], [ret_b["qdT"]])
                self.copy("act", rt["kT_sb"][:, :], v[:, W4:2 * W4], [ptb], [ret_b["kT_sb"]])
                for h in range(RH):
                    self.ts("dve", rt["kd"][:, h * P:(h + 1) * P], rk_tm[sl][:, h * P:(h + 1) * P], kdec[:, h:h + 1], None, ALU.mult, None,
                            [zq_b, const_b], [ret_b["kd"]])
                yield
                ps, psb = self.bank()
                for h in range(RH):
                    self.mm(ps[:, h * P:(h + 1) * P], rt["kT_sb"][:, h * P:(h + 1) * P], rt["qT_sb"][:, h * P:(h + 1) * P], True, True,
                            [ret_b["kT_sb"], ret_b["qT_sb"]], [psb])
                self.tt("dve", rt["sTm"][:, :], ps[:, 0:W4], maskT[:, :, :].rearrange("p h i -> p (h i)"), ALU.mult, [psb, const_b], [ret_b["sTm"]])
                yield
                for b2 in range(2):
                    po_, pob_ = self.bank()
                    for hh in range(2):
                        h = b2 * 2 + hh
                        self.mm(po_[:, hh * RDV:(hh + 1) * RDV], rt["sTm"][:, h * P:(h + 1) * P], rv_tm[sl][:, h * RDV:(h + 1) * RDV], True, False,
                                [ret_b["sTm"], zv_b], [pob_])
                        self.mm(po_[:, hh * RDV:(hh + 1) * RDV], rt["qdT"][:, h * P:(h + 1) * P], S_bf[:, h, :], False, True,
                                [ret_b["qdT"], Sbf_b], [pob_])
                    self.copy("act", cens[ci][:, b2 * 2:(b2 + 1) * 2, :], po_[:, 0:2 * RDV].rearrange("p (h e) -> p h e", h=2), [pob_], [cens_b[ci]])
                yield
                for b2 in range(2):
                    ps2, ps2b = self.bank()
                    for hh in range(2):
                        h = b2 * 2 + hh
                        self.mm(ps2[:, hh * RDV:(hh + 1) * RDV], rt["kd"][:, h * P:(h + 1) * P], rv_tm[sl][:, h * RDV:(h + 1) * RDV], True, True,
                                [ret_b["kd"], zv_b], [ps2b])
                    for hh in range(2):
                        h = b2 * 2 + hh
                        self.stt("dve", S[:, h, :], S[:, h, :], float(np.exp(lg[h] * P)), ps2[:, hh * RDV:(hh + 1) * RDV], ALU.mult, ALU.add,
                                 [S_b, ps2b], [S_b])
                self.copy("act", S_bf[:, :, :], S[:, :, :], [S_b], [Sbf_b])
                yield
                groupnorm_gate(t, ci)
                yield

        def retention_side(pas):
            t = pas.tiles[0]
            r = NSIDE
            self.memset("pool", qTm[:, :, :, :], 0.0, [], [qTm_b])
            for h in range(RH):
                pt, ptb = self.bank()
                v = bfv(pt)
                self.tr(v[:, 0:r], rq_tm[0][0:r, h * P:(h + 1) * P], ident_b[0:r, 0:r], [zq_b, const_b], [ptb])
                for s in range(NSMP):
                    self.copy("dve", qTm[:, h, s, NMETA + s:NMETA + s + 1], v[:, NMETA + s:NMETA + s + 1], [ptb], [qTm_b])
                self.ts("pool", rt["kd"][0:r, 0:P], rk_tm[0][0:r, h * P:(h + 1) * P], kdec[0:r, 4 + h:5 + h], None, ALU.mult, None,
                        [zq_b, const_b], [ret_b["kd"]])
                ps2, ps2b = self.bank()
                self.mm(ps2[:, 0:RDV], rt["kd"][0:r, 0:P], rv_tm[0][0:r, h * RDV:(h + 1) * RDV], True, True, [ret_b["kd"], zv_b], [ps2b])
                self.copy("dve", S[:, h, :], ps2[:, 0:RDV], [ps2b], [S_b])
                self.copy("act", S_bf[:, h, :], S[:, h, :], [S_b], [Sbf_b])
                yield
            for h in range(RH):
                for s in range(NSMP):
                    self.dma(Sp[:, :], state[s, h], [], [sret_b], "sp_in")
                    self.ts("dve", vm[0:r, :], rv_tm[0][0:r, h * RDV:(h + 1) * RDV], onehot[0:r, s:s + 1], None, ALU.mult, None,
                            [zv_b, const_b, sret_b], [sret_b])
                    ps, psb = self.bank()
                    self.mm(ps[:, 0:RDV], rk_tm[0][0:r, h * P:(h + 1) * P], vm[0:r, :], True, True, [zq_b, sret_b], [psb])
                    self.stt("dve", Sn[:, :], Sp[:, :], float(np.exp(lg[h])), ps[:, 0:RDV], ALU.mult, ALU.add, [sret_b, psb], [sret_b])
                    self.copy("act", Snb[:, :], Sn[:, :], [sret_b], [sret_b])
                    self.dma(S_s[s, h], Sn[:, :], [sret_b], [], "o_Ss")
                    po_, pob_ = self.bank()
                    self.mm(po_[0:r, 0:RDV], qTm[:, h, s, :], Snb[:, :], True, True, [qTm_b, sret_b], [pob_])
                    if s == 0:
                        self.copy("dve", cens[0][0:r, h, :], po_[0:r, 0:RDV], [pob_], [cens_b[0]])
                    else:
                        self.tt("dve", cens[0][0:r, h, :], po_[0:r, 0:RDV], cens[0][0:r, h, :], ALU.add, [pob_, cens_b[0]], [cens_b[0]])
                    yield
            groupnorm_gate(t, 0)
            yield

        def mla_samples(pas):
            regs = self.sp_regs
            for rk in range(2):
                ps, psb = self.bank()
                for h in range(HM):
                    self.mm(ps[:, h * NSIDE:(h + 1) * NSIDE], w_ukT[0:NOPE, h, rk * P:(rk + 1) * P], qT[0:NOPE, h, 0:NSIDE], True, True,
                            [wres_b, qT_b], [psb])
                self.copy("act", qlatT[:, rk, :, :], ps[:, 0:HM * NSIDE].rearrange("p (h j) -> p h j", h=HM), [psb], [smp_b])
            self.memset("dve", lsum[:, :], 0.0, [], lsum_b)
            pq, pqb = self.bank()
            vq = bfv(pq)
            for h in range(HM):
                self.tr(vq[0:ROPE, h * NSIDE:(h + 1) * NSIDE], q_tm[0:NSIDE, 0, h, NOPE:HD], ident_b[0:NSIDE, 0:NSIDE],
                        [q_tm_b, const_b], [pqb])
            self.copy("act", qropeT[:, :, :], vq[0:ROPE, 0:HM * NSIDE].rearrange("p (h j) -> p h j", h=HM), [pqb], [smp_b])
            nblk = npages // 4
            bi = 0
            accs = [self.rbank(4 + s) for s in range(NSMP)]
            for b in range(nblk):
                for s in range(NSMP):
                    acc, accb = accs[s]
                    col = NMETA + s
                    i2 = bi % NNAT
                    st_ = bi % NSET
                    bi += 1
                    nc_, nk_ = natc[i2], natk[i2]
                    sb_ = sblk_b[st_]
                    icol = s * nblk + b

                    def mk(dst, src_rows, icol=icol):
                        def fn(e):
                            return e.indirect_dma_start(out=dst, out_offset=None, in_=src_rows,
                                                        in_offset=bass.IndirectOffsetOnAxis(ap=pidx_sb[:, icol:icol + 1], axis=0))
                        return fn
                    self.S.add("pool", mk(nc_[:, :, :].rearrange("p u c -> p (u c)"), ckv_rows), [ptab_b], [natb_b[i2]], dma_sem=self.dsem(f"natb{i2}"))
                    self.S.add("pool", mk(nk_[:, :, :].rearrange("p u c -> p (u c)"), kr_rows), [ptab_b], [natb_b[i2]], dma_sem=self.dsem(f"natb{i2}"))
                    pa, pab = self.bank()
                    va = bfv(pa)
                    for pg in range(4):
                        for rk in range(2):
                            self.tr(va[:, rk * 512 + pg * P:rk * 512 + (pg + 1) * P], nc_[:, pg, rk * P:(rk + 1) * P], ident_b[:, :],
                                    [natb_b[i2], const_b], [pab])
                    pb_, pbb = self.bank()
                    vb = bfv(pb_)
                    for pg in range(4):
                        self.tr(vb[0:ROPE, pg * P:(pg + 1) * P], nk_[:, pg, :], ident_b[:, :], [natb_b[i2], const_b], [pbb])
                    self.copy("dve", ckvT_sb[st_][:, :], va[:, 0:1024], [pab], [sb_["ckvT_sb"]])
                    self.copy("act", krT_sb[st_][0:ROPE, :], vb[0:ROPE, 0:512], [pbb], [sb_["krT_sb"]])
                    pss, pssb = self.bank()
                    self.mm(pss[0:8, 0:512], qlatT[:, 0, :, col], ckvT_sb[st_][:, 0:512], True, False, [smp_b, sb_["ckvT_sb"]], [pssb])
                    self.mm(pss[0:8, 0:512], qlatT[:, 1, :, col], ckvT_sb[st_][:, 512:1024], False, False, [smp_b, sb_["ckvT_sb"]], [pssb])
                    self.mm(pss[0:8, 0:512], qropeT[:, :, col], krT_sb[st_][0:ROPE, :], False, True, [smp_b, sb_["krT_sb"]], [pssb])
                    mcol = 24 + 2 * s
                    if b == 0:
                        self.S.add("dve", lambda e, pss=pss, mcol=mcol: e.reduce_max(out=stat2[0:8, mcol:mcol + 1], in_=pss[0:8, 0:512], axis=AX.X),
                                   [pssb], [stat2_b])
                        self.ts("dve", stat2[0:8, mcol + 1:mcol + 2], stat2[0:8, mcol:mcol + 1], -MLA_SCALE, None, ALU.mult, None, [stat2_b], [stat2_b])
                    self.activation(p_sb[st_][0:8, :], pss[0:8, 0:512], AF.Exp, [pssb, stat2_b], [sb_["p_sb"], lsum_b[s]],
                                    bias=stat2[0:8, mcol + 1:mcol + 2], scale=MLA_SCALE,
                                    accum_out=lsum[0:8, s * NLS + b:s * NLS + b + 1])
                    pp, ppb = self.bank()
                    vp = bfv(pp)
                    for pg in range(4):
                        self.tr(vp[:, pg * 8:(pg + 1) * 8], p_sb[st_][0:8, pg * P:(pg + 1) * P], ident_b[0:8, 0:8], [sb_["p_sb"], const_b], [ppb])
                    self.copy("dve", pT_sb[st_][:, 0:32], vp[:, 0:32], [ppb], [sb_["pT_sb"]])
                    for pg in range(4):
                        self.mm(acc[0:8, 0:256], pT_sb[st_][:, pg * 8:(pg + 1) * 8], nc_[:, pg, :], b == 0 and pg == 0, False,
                                [sb_["pT_sb"], natb_b[i2]], [accb])
                    yield
            for s in range(NSMP):
                acc, accb = accs[s]
                col = NMETA + s
                mcol = 24 + 2 * s
                pss, pssb = self.bank()
                self.mm(pss[0:8, 0:NSIDE], qlatT[:, 0, :, col], ckvTs[:, 0, :], True, False, [smp_b, side_b], [pssb])
                self.mm(pss[0:8, 0:NSIDE], qlatT[:, 1, :, col], ckvTs[:, 1, :], False, False, [smp_b, side_b], [pssb])
                self.mm(pss[0:8, 0:NSIDE], qT[64:96, :, col], krTs[64:96, 0:NSIDE], False, True, [qT_b, side_b], [pssb])
                self.activation(p20[0:8, :], pss[0:8, 0:NSIDE], AF.Exp, [pssb, stat2_b], [smp_b], bias=stat2[0:8, mcol + 1:mcol + 2], scale=MLA_SCALE)
                self.tt("dve", p20b[0:8, :], p20[0:8, :], smask[0:8, s * NSIDE:(s + 1) * NSIDE], ALU.mult, [smp_b, const_b], [smp_b])
                pp, ppb = self.bank()
                vp = bfv(pp)
                self.tr(vp[0:NSIDE, 0:8], p20b[0:8, 0:NSIDE], ident_b[0:8, 0:8], [smp_b, const_b], [ppb])
                self.copy("dve", pT20[0:NSIDE, 0:8], vp[0:NSIDE, 0:8], [ppb], [smp_b])
                self.mm(acc[0:8, 0:256], pT20[0:NSIDE, 0:8], ckvs_bf[0:NSIDE, 2:258], False, True, [smp_b, side_b], [accb])
                self.S.add("dve", lambda e, s=s: e.reduce_sum(out=lsum[0:8, s * NLS + nblk:s * NLS + nblk + 1], in_=p20b[0:8, :], axis=AX.X),
                           [smp_b], [lsum_b[s]])
                self.S.add("dve", lambda e, s=s: e.reduce_sum(out=stat2[0:8, s:s + 1], in_=lsum[0:8, s * NLS:(s + 1) * NLS], axis=AX.X),
                           [lsum_b[s]], [stat2_b])
                self.recip(stat2[0:8, s:s + 1], stat2[0:8, s:s + 1], [stat2_b], [stat2_b])
                self.ts("dve", ol_sb[0:8, :], acc[0:8, 0:256], stat2[0:8, s:s + 1], None, ALU.mult, None, [accb, stat2_b], [smp_b])
                pt, ptb = self.bank()
                v = bfv(pt)
                for rk in range(2):
                    self.tr(v[:, rk * 8:(rk + 1) * 8], ol_sb[0:8, rk * P:(rk + 1) * P], ident_b[0:8, 0:8], [smp_b, const_b], [ptb])
                self.copy("dve", olT[:, :, s * 8:(s + 1) * 8], v[:, 0:16].rearrange("p (k j) -> p k j", k=2), [ptb], [smp_b])
            ps, psb = self.bank()
            for rk in range(2):
                self.mm(ps[0:32, 0:512], olT[:, rk, :], w_uv_sb[:, rk, :], rk == 0, rk == 1, [smp_b, wres_b], [psb])
            self.tt("dve", am[0:32, :], ps[0:32, 0:512], bmask[:, :], ALU.mult, [psb, const_b], [smp_b])
            ps2, ps2b = self.bank()
            self.mm(ps2[0:NSIDE, 0:512], sel2[:, :], am[0:32, :], True, True, [const_b, smp_b], [ps2b])
            self.copy("act", a_tm[0:NSIDE, 0, :], ps2[0:NSIDE, 0:512], [ps2b], [a_tm_b])
            self.memset("pool", stat[0:1, 30:31], 0.0,
                        natb_b + [b_ for d_ in sblk_b for b_ in d_.values()] + [smp_b], [zq_b, zv_b, zg_b, stat_b])

        def mixer_out(pas):
            n = pas.n
            for gdst, goff, gb_ in ((ga_s, OFF_GA, zq_b), (gb_s, OFF_GB, zv_b)):
                for half in range(2):
                    def ev_g(t, ps, psb, half=half, gdst=gdst, gb_=gb_):
                        self.activation(gdst[t.slot][0:t.rows, half * 512:(half + 1) * 512], ps[0:t.rows, 0:512], AF.Sigmoid, [psb], [gb_])
                    proj_tm(pas, uT, uT_b, KD, w_in, goff + half * 512, 512, ev_g, allow_hi=True)
            for half in range(2):
                def ev_a(t, ps, psb, half=half):
                    self.tt("dve", m_tm[t.slot][0:t.rows, half * 512:(half + 1) * 512], ps[0:t.rows, 0:512],
                            ga_s[t.slot][0:t.rows, half * 512:(half + 1) * 512], ALU.mult, [psb, zq_b], [zg_b])
                proj_tm(pas, aT, aT_b, 4, w_mla_o, half * 512, 512, ev_a, allow_hi=True)
            for half in range(2):
                def ev_r(t, ps, psb, half=half):
                    r = t.rows
                    self.tt("dve", ma[0:r, :], ps[0:r, 0:512], gb_s[t.slot][0:r, half * 512:(half + 1) * 512], ALU.mult, [psb, zv_b], [ma_b])
                    self.tt("pool", m_tm[t.slot][0:r, half * 512:(half + 1) * 512], ma[0:r, :],
                            m_tm[t.slot][0:r, half * 512:(half + 1) * 512], ALU.add, [ma_b, zg_b], [zg_b])
                proj_tm(pas, rinT, rinT_b, KD, w_ret_o, half * 512, 512, ev_r, allow_hi=True)
            for t in pas.tiles:
                r = t.rows
                pt, ptb = self.bank()
                v = bfv(pt)
                for k8 in range(KD):
                    self.tr(v[:, k8 * P:k8 * P + r], m_tm[t.slot][0:r, k8 * P:(k8 + 1) * P], ident_b[0:r, 0:r], [zg_b, const_b], [ptb])
                evac_copy(mT[:, :, t.c0:t.c0 + r], v[:, 0:1024].rearrange("p (k c) -> p k c", k=KD)[:, :, 0:r], [ptb], [mT_b])
            for ig in range(4):
                w, wbuf = self.load_w(w_out[:, ig * 256:(ig + 1) * 256].rearrange("(k p) c -> p k c", p=P), KD, 256)
                for ii in range(2):
                    i = ig * 2 + ii
                    ps, psb = self.bank()
                    for k in range(KD):
                        self.mm(ps[:, 0:n], w[:, k, ii * P:(ii + 1) * P], mT[:, k, 0:n], k == 0, k == KD - 1, [wbuf, mT_b], [psb])
                    self.tt("dve", hT[:, i, 0:n], ps[:, 0:n], hT[:, i, 0:n], ALU.add, [psb, hT_b], [hT_b])

        self.reg_i = 0
        passes = [mk_pass("side", 0)] + [mk_pass("main", c) for c in range(nch)]
        try:
            load_x_dma(passes[0])
            for pi_, pas in enumerate(passes):
                load_x(pas)
                chk(pas.kind + ":load")
                ffn(pas.n, 0, W["g1"], W["u1"], W["d1"])
                chk(pas.kind + ":ffn1")
                mixer_proj(pas)
                chk(pas.kind + ":proj")
                if pas.kind == "side":
                    gs_ = mla_samples(pas)
                    gr_ = retention_side(pas)
                    nb_tot = NSMP * (npages // 4)
                    every = max(1, nb_tot // 24)
                    ib = 0
                    done_r = False
                    for _ in gs_:
                        ib += 1
                        if not done_r and ib % every == 0:
                            try:
                                next(gr_)
                            except StopIteration:
                                done_r = True
                    for _ in gr_:
                        pass
                else:
                    ga_ = mla_main(pas)
                    gr_ = retention_main(pas)
                    n_att = HM * (3 + 4 * (pas.c + 1) + 1)
                    n_ret = 4 * 5
                    done_a = done_r = False
                    ia = ir = 0
                    while not (done_a and done_r):
                        if not done_a and (done_r or ia * n_ret <= ir * n_att):
                            try:
                                next(ga_)
                                ia += 1
                            except StopIteration:
                                done_a = True
                        elif not done_r:
                            try:
                                next(gr_)
                                ir += 1
                            except StopIteration:
                                done_r = True
                a_transpose(pas)
                mixer_out(pas)
                chk(pas.kind + ":mixout")
                ffn(pas.n, 2, W["g2"], W["u2"], W["d2"])
                if pi_ + 1 < len(passes):
                    load_x_dma(passes[pi_ + 1])
                final_norm_out(pas)
                chk(pas.kind + ":end")
        except _Stop:
            pass
        for h in range(RH):
            self.dma(S_p[h], S[:, h, :], [S_b], [], "o_Sp")

        self.S.finalize()
        with contextlib.ExitStack() as es2:
            sems = {e: es2.enter_context(nc.semaphore(f"s_{e}")) for e in COMPUTE}
            dsems = {n_: es2.enter_context(nc.semaphore(f"d_{n_}")) for n_ in self.dma_sem_names}
            finals = [(n_, v_) for n_, v_ in self.S.dma_counts.items() if n_.startswith("o_")]
            self.sp_regs_ctx = es2
            self.S.emit(nc, sems, dsems, finals)
        self.es.close()
        return nc

    def _alloc_regs(self, e):
        self.sp_regs.clear()
        for i in range(4):
            self.sp_regs.append(self.sp_regs_ctx.enter_context(e.register(f"pg{i}")))


def host_consts(nch=4, npages=NPAGES):
    NTT = 1 + nch * 4
    f32 = np.float32
    c = {}
    c["c_ident"] = np.eye(P, dtype=f32)
    c["c_iota"] = (np.arange(P) % 32).astype(f32)[:, None].copy()
    pos = np.zeros((NTT, P), dtype=np.int64)
    pos[0, :NMETA] = np.arange(NMETA)
    pos[0, NMETA:NSIDE] = npages * PAGE
    for gt in range(1, NTT):
        pos[gt] = NMETA + (gt - 1) * P + np.arange(P)
    for nm, half in (("R", 64), ("M", 16)):
        inv = (f32(10000.0) ** (-(np.arange(half, dtype=f32)) / f32(half))).astype(f32)
        ang = (pos.astype(f32)[:, :, None] * inv[None, None, :]).astype(f32)
        c["c_cos" + nm] = np.cos(ang).astype(f32)
        c["c_sin" + nm] = np.sin(ang).astype(f32)
    lg = np.log1p(-np.exp2(-5.0 - np.arange(RH))).astype(np.float64)
    idx = np.arange(P, dtype=np.float64)
    diff = idx[None, :] - idx[:, None]
    maskT = np.zeros((P, RH, P), dtype=f32)
    qdec = np.zeros((P, RH, P), dtype=f32)
    kdec = np.zeros((P, 8), dtype=f32)
    for h in range(RH):
        maskT[:, h, :] = np.where(diff >= 0, np.exp(lg[h] * np.maximum(diff, 0.0)), 0.0)
        qdec[:, h, :] = np.exp(lg[h] * (idx + 1.0))[None, :]
        kdec[:, h] = np.exp(lg[h] * (P - 1.0 - idx))
        kdec[:NMETA, 4 + h] = np.exp(lg[h] * (NMETA - 1.0 - idx[:NMETA]))
    c["c_maskT"], c["c_qdec"], c["c_kdec"] = maskT, qdec, kdec
    c["c_tri"] = (idx[:, None] <= idx[None, :]).astype(f32)
    oh = np.zeros((P, NSMP), dtype=f32)
    bm = np.zeros((32, 512), dtype=f32)
    sel2 = np.zeros((32, NSIDE), dtype=f32)
    sm = np.zeros((8, NSMP * NSIDE), dtype=f32)
    for s in range(NSMP):
        oh[NMETA + s, s] = 1.0
        sm[:, s * NSIDE + NMETA + s] = 1.0
        for h in range(HM):
            bm[s * 8 + h, h * VD:(h + 1) * VD] = 1.0
            sel2[s * 8 + h, NMETA + s] = 1.0
    c["c_onehot"], c["c_bmask"], c["c_sel2"], c["c_smask"] = oh, bm, sel2, sm
    return c


_WNAMES = ["ffn1_norm", "ffn1_gate", "ffn1_up", "ffn1_down", "mix_norm", "w_in", "q_norm", "kv_norm", "w_uq",
           "w_mla_o", "ret_gn", "w_ret_o", "w_out", "ffn2_norm", "ffn2_gate", "ffn2_up", "ffn2_down"]


def make_in_maps(inputs, ncores, nch, npages):
    f32 = np.float32
    consts = host_consts(nch, npages)
    shared = {}
    for n_ in _WNAMES:
        shared[n_] = np.ascontiguousarray(np.asarray(inputs[n_], dtype=f32)[0])
    shared["w_uk"] = np.ascontiguousarray(np.asarray(inputs["w_uk"], dtype=f32)[0].reshape(KVR, HM * NOPE))
    shared["w_uv"] = np.ascontiguousarray(np.asarray(inputs["w_uv"], dtype=f32)[0].reshape(KVR, HM * VD))
    shared["final_norm"] = np.ascontiguousarray(np.asarray(inputs["final_norm"], dtype=f32))
    shared["meta"] = np.ascontiguousarray(np.asarray(inputs["meta_tokens"], dtype=f32))
    shared["cache_ckv"] = np.ascontiguousarray(np.asarray(inputs["cache_ckv"], dtype=f32)[0])
    shared["cache_krope"] = np.ascontiguousarray(np.asarray(inputs["cache_krope"], dtype=f32)[0])
    shared.update(consts)
    xp = np.asarray(inputs["x_prompt"], dtype=f32)
    xsm = np.asarray(inputs["x_sample"], dtype=f32)
    st = np.asarray(inputs["state_ret"], dtype=f32)[0]
    pt = np.asarray(inputs["page_table"], dtype=np.int32)
    maps = []
    for c in range(ncores):
        m = dict(shared)
        m["x"] = np.ascontiguousarray(xp[c])
        m["xs"] = np.ascontiguousarray(xsm[c * NSMP:(c + 1) * NSMP, 0, :])
        m["state"] = np.ascontiguousarray(st[c * NSMP:(c + 1) * NSMP])
        m["ptab"] = np.ascontiguousarray(pt[c * NSMP:(c + 1) * NSMP])
        maps.append(m)
    return maps


def gather_outputs(results, ncores):
    f32 = np.float32
    y = np.stack([r["y"] for r in results]).astype(f32)
    ys = np.concatenate([r["ys"] for r in results])[:, None, :].astype(f32)
    ckv_p = np.stack([r["ckv_p"] for r in results])[None].astype(f32)
    kr_p = np.stack([r["kr_p"] for r in results])[None].astype(f32)
    S_p = np.stack([r["S_p"] for r in results])[None].astype(f32)
    ckv_s = np.concatenate([r["ckv_s"] for r in results])[None, :, None, :].astype(f32)
    kr_s = np.concatenate([r["kr_s"] for r in results])[None, :, None, :].astype(f32)
    S_s = np.concatenate([r["S_s"] for r in results])[None].astype(f32)
    return (y, ys, ckv_p, kr_p, S_p, ckv_s, kr_s, S_s)


def kernel(**inputs):
    nch = SEQ // CH
    b = Builder(dict(nch=nch, npages=NPAGES))
    nc = b.build()
    maps = make_in_maps(inputs, NCORES, nch, NPAGES)
    res = run_bass_kernel_spmd(nc, maps, core_ids=list(range(NCORES)))
    return gather_outputs(res.results, NCORES)
```

```python
import contextlib
import numpy as np
import concourse.bass as bass
import concourse.mybir as mybir
from concourse.bass_utils import run_bass_kernel_spmd

F32 = mybir.dt.float32
BF16 = mybir.dt.bfloat16
I32 = mybir.dt.int32
ALU = mybir.AluOpType
AF = mybir.ActivationFunctionType
AX = mybir.AxisListType

P = 128
D = 1024
KD = 8
FF = 2816
KF = 22
NMETA = 16
NSMP = 4
NSIDE = NMETA + NSMP
CH = 512
SEQ = 2048
NCORES = 8
QR = 384
KVR = 256
ROPE = 32
NOPE = 64
HM = 8
HD = NOPE + ROPE
VD = 64
RH = 4
RDK = 128
RDV = 256
PAGE = 128
NPAGES = 128
PAST = NPAGES * PAGE
NPOOL = 5120
EPS = 1e-6
MLA_SCALE = float(HD) ** -0.5
IN_DIM = QR + KVR + ROPE + 2 * RH * RDK + 2 * RH * RDV + 2 * D
OFF_CQ = 0
OFF_CKV = QR
OFF_KR = QR + KVR
OFF_RQ = OFF_KR + ROPE
OFF_RK = OFF_RQ + RH * RDK
OFF_RV = OFF_RK + RH * RDK
OFF_RG = OFF_RV + RH * RDV
OFF_GA = OFF_RG + RH * RDV
OFF_GB = OFF_GA + D


class Buf:
    __slots__ = ("name", "w", "r", "excl")

    def __init__(self, name, excl=False):
        self.name = name
        self.w = None
        self.r = []
        self.excl = excl


class Op:
    __slots__ = ("eng", "fn", "deps", "dma_sem", "dma_val", "signal", "count", "idx", "seq", "todo", "known")

    def __init__(self, eng, fn):
        self.eng = eng
        self.fn = fn
        self.deps = []
        self.dma_sem = None
        self.dma_val = 0
        self.signal = False
        self.count = 0
        self.idx = 0


COMPUTE = ("pe", "act", "dve", "pool")


class Sched:
    def __init__(self):
        self.ops = {e: [] for e in ("pe", "act", "dve", "pool", "sp")}
        self.dma_counts = {}
        self.out_dmas = []
        self.all_ops = []

    def add(self, eng, fn, reads=(), writes=(), dma_sem=None, nodep_same_pe=True):
        op = Op(eng, fn)
        deps = {}
        for b in reads:
            if b.w is not None:
                deps[id(b.w)] = b.w
            if b.excl:
                for r in b.r:
                    if r.eng != eng:
                        deps[id(r)] = r
        for b in writes:
            if b.w is not None:
                deps[id(b.w)] = b.w
            for r in b.r:
                deps[id(r)] = r
        for d in deps.values():
            if d is op:
                continue
            if d.eng == "pe" and eng == "pe":
                continue
            op.deps.append(d)
        for b in reads:
            b.r.append(op)
        for b in writes:
            b.w = op
            b.r = []
        if dma_sem is not None:
            op.dma_sem = dma_sem
            self.dma_counts[dma_sem] = self.dma_counts.get(dma_sem, 0) + 16
            op.dma_val = self.dma_counts[dma_sem]
        op.idx = len(self.ops[eng])
        self.ops[eng].append(op)
        op.seq = len(self.all_ops)
        self.all_ops.append(op)
        return op

    def finalize(self):
        for e, lst in self.ops.items():
            for op in lst:
                for d in op.deps:
                    if d.dma_sem is None:
                        d.signal = True
        for e in COMPUTE:
            c = 0
            for op in self.ops[e]:
                if op.signal:
                    c += 1
                    op.count = c
        waited = {e: {} for e in self.ops}
        for op in self.all_ops:
            w = waited[op.eng]
            need = {}
            for d in op.deps:
                key = ("d", d.dma_sem) if d.dma_sem is not None else ("e", d.eng)
                val = d.dma_val if d.dma_sem is not None else d.count
                if key not in need or need[key][0] < val:
                    need[key] = (val, d)
            todo = []
            for key, (val, d) in sorted(need.items(), key=lambda kv: -kv[1][1].seq):
                if w.get(key, 0) < val:
                    todo.append((key, val))
                    w[key] = val
                for k2, v2 in d.known.items():
                    if w.get(k2, 0) < v2:
                        w[k2] = v2
            op.todo = todo
            op.known = dict(w)

    def emit(self, nc, sems, dma_sems, final_waits, pre_sp=None):
        engobj = {"pe": "tensor", "act": "scalar", "dve": "vector", "pool": "gpsimd", "sp": "sync"}

        def run(ename, eng):
            if ename == "sp" and pre_sp is not None:
                pre_sp(eng)
            waited = {}
            for op in self.ops[ename]:
                todo = [(dma_sems[key[1]] if key[0] == "d" else sems[key[1]], val) for key, val in op.todo]
                for sem, val in todo[:-1]:
                    eng.wait_ge(sem, val)
                ins = op.fn(eng)
                if todo:
                    ins._wait_ge(todo[-1][0], todo[-1][1])
                if op.dma_sem is not None:
                    ins.then_inc(dma_sems[op.dma_sem], 16)
                elif op.signal:
                    ins.then_inc(sems[ename], 1)
            if ename == "sp":
                for name, val in final_waits:
                    eng.wait_ge(dma_sems[name], val)

        with nc.Block() as block:
            @block.sync
            def _(e):
                run("sp", e)

            @block.tensor
            def _(e):
                run("pe", e)

            @block.scalar
            def _(e):
                run("act", e)

            @block.vector
            def _(e):
                run("dve", e)

            @block.gpsimd
            def _(e):
                run("pool", e)


class Builder:
    def __init__(self, cfg):
        self.cfg = cfg
        self.nch = cfg.get("nch", 4)
        self.npages = cfg.get("npages", NPAGES)
        self.stages = cfg.get("stages", "full")
        self.nc = bass.Bass("TRN2", target_bir_lowering=False)
        self.S = Sched()
        self.es = contextlib.ExitStack()
        self.dma_sem_names = []
        self.n_act_dve = 0
        self.regs = []
        self.sp_regs = []
        self.npool = cfg.get('npool', NPOOL)

    def sb(self, name, shape, dt):
        t = self.es.enter_context(self.nc.sbuf_tensor(name, list(shape), dt))
        return t

    def dram_in(self, name, shape, dt=F32):
        return self.nc.dram_tensor(name, list(shape), dt, kind="ExternalInput").ap()

    def dram_out(self, name, shape, dt=F32):
        return self.nc.dram_tensor(name, list(shape), dt, kind="ExternalOutput").ap()

    def dsem(self, name):
        if name not in self.dma_sem_names:
            self.dma_sem_names.append(name)
        return name

    def pe(self, fn, reads, writes):
        return self.S.add("pe", fn, reads, writes)

    def act(self, fn, reads, writes):
        return self.S.add("act", fn, reads, writes)

    def dve(self, fn, reads, writes):
        return self.S.add("dve", fn, reads, writes)

    def pool(self, fn, reads, writes):
        return self.S.add("dve", fn, reads, writes)

    def anyv(self, fn, reads, writes):
        self.n_act_dve += 1
        if self.n_act_dve % 3 == 0:
            return self.pool(fn, reads, writes)
        return self.dve(fn, reads, writes)

    def dma(self, out, in_, reads, writes, sem, **kw):
        self.dsem(sem)
        return self.S.add("sp", lambda e: e.dma_start(out=out, in_=in_, **kw), reads, writes, dma_sem=sem)

    def activation(self, out, in_, func, reads, writes, eng="act", **kw):
        return self.S.add(eng, lambda e: e.activation(out=out, in_=in_, func=func, **kw), reads, writes)

    def tt(self, eng, out, in0, in1, op, reads, writes):
        eng = "dve" if eng == "pool" else eng
        return self.S.add(eng, lambda e: e.tensor_tensor(out=out, in0=in0, in1=in1, op=op), reads, writes)

    def stt(self, eng, out, in0, scalar, in1, op0, op1, reads, writes):
        eng = "dve" if eng == "pool" else eng
        return self.S.add(eng, lambda e: e.scalar_tensor_tensor(out=out, in0=in0, scalar=scalar, in1=in1,
                                                                 op0=op0, op1=op1), reads, writes)

    def ts(self, eng, out, in0, s1, s2, op0, op1, reads, writes):
        eng = "dve" if eng == "pool" else eng
        if op1 is None:
            return self.S.add(eng, lambda e: e.tensor_scalar(out=out, in0=in0, scalar1=s1, scalar2=None, op0=op0),
                              reads, writes)
        return self.S.add(eng, lambda e: e.tensor_scalar(out=out, in0=in0, scalar1=s1, scalar2=s2, op0=op0, op1=op1),
                          reads, writes)

    def copy(self, eng, out, in_, reads, writes):
        eng = "dve" if eng == "pool" else eng
        if eng == "act":
            return self.S.add(eng, lambda e: e.activation(out=out, in_=in_, func=AF.Copy), reads, writes)
        return self.S.add(eng, lambda e: e.tensor_copy(out=out, in_=in_), reads, writes)

    def recip(self, out, in_, reads, writes):
        return self.S.add("dve", lambda e: e.reciprocal(out=out, in_=in_), reads, writes)

    def memset(self, eng, ap, val, reads, writes):
        eng = "dve" if eng == "pool" else eng
        return self.S.add(eng, lambda e: e.memset(ap, val), reads, writes)

    def mm(self, out, lhsT, rhs, start, stop, reads, writes):
        return self.pe(lambda e: e.matmul(out, lhsT, rhs, start=start, stop=stop), reads, writes)

    def tr(self, out, in_, ident, reads, writes):
        return self.pe(lambda e: e.transpose(out, in_, ident), reads, writes)

    def dbg(self, name, ap, shape, buf, dt=F32):
        if not self.cfg.get("debug"):
            return
        o = self.dram_out("dbg_" + name, shape, dt)
        self.dma(o, ap, [buf], [], self.dsem("o_dbg_" + name))

    def init_psum(self):
        self.banks = []
        self.bank_bufs = []
        for i in range(8):
            t = self.es.enter_context(self.nc.psum_tensor(f"ps{i}", [P, 512], F32))
            self.banks.append(t)
            self.bank_bufs.append(Buf(f"ps{i}", excl=True))
        self.bank_i = 0
        self.bank2_i = 0

    NROT = 4

    def bank(self):
        i = self.bank_i
        self.bank_i = (i + 1) % self.NROT
        return self.banks[i], self.bank_bufs[i]

    def bank2(self):
        i = 4 + self.bank2_i
        self.bank2_i = (self.bank2_i + 1) % 4
        return self.banks[i], self.bank_bufs[i]

    def rbank(self, i):
        return self.banks[i], self.bank_bufs[i]

    WSLOT = 2048

    def init_wring(self):
        self.nwb = 6
        self.wb = [self.sb(f"wb{i}", [P, self.WSLOT], BF16) for i in range(self.nwb)]
        self.wb_b = [Buf(f"wb{i}") for i in range(self.nwb)]
        self.wb_i = 0

    def load_w(self, src, a, b, rows=P):
        assert a * b <= self.WSLOT
        r = self.wb_i
        self.wb_i = (r + 1) % self.nwb
        wv = self.wb[r][0:rows, 0:a * b].rearrange("p (a b) -> p a b", a=a)
        self.dsem(f"wb{r}")
        self.S.add("pool", lambda e: e.dma_start(out=wv, in_=src), [], [self.wb_b[r]], dma_sem=f"wb{r}")
        return wv, self.wb_b[r]

    def build(self):
        nc = self.nc
        nch = self.nch
        npages = self.npages
        T = NMETA + nch * CH
        NTT = 1 + nch * 4
        B = Buf

        x = self.dram_in("x", [nch * CH, D])
        xs = self.dram_in("xs", [NSMP, D])
        meta = self.dram_in("meta", [NMETA, D])
        W = {}
        for l in ("1", "2"):
            W["n" + l] = self.dram_in(f"ffn{l}_norm", [D])
            W["g" + l] = self.dram_in(f"ffn{l}_gate", [D, FF])
            W["u" + l] = self.dram_in(f"ffn{l}_up", [D, FF])
            W["d" + l] = self.dram_in(f"ffn{l}_down", [FF, D])
        W["fn"] = self.dram_in("final_norm", [D])
        W["mixn"] = self.dram_in("mix_norm", [D])
        w_in = self.dram_in("w_in", [D, IN_DIM])
        q_norm = self.dram_in("q_norm", [QR])
        kv_norm = self.dram_in("kv_norm", [KVR])
        w_uq = self.dram_in("w_uq", [QR, HM * HD])
        w_uk = self.dram_in("w_uk", [KVR, HM * NOPE])
        w_uv = self.dram_in("w_uv", [KVR, HM * VD])
        w_mla_o = self.dram_in("w_mla_o", [HM * VD, D])
        ret_gn = self.dram_in("ret_gn", [RH * RDV])
        w_ret_o = self.dram_in("w_ret_o", [RH * RDV, D])
        w_out = self.dram_in("w_out", [D, D])
        cache_ckv = self.dram_in("cache_ckv", [self.npool, PAGE, KVR])
        cache_kr = self.dram_in("cache_krope", [self.npool, PAGE, ROPE])
        state = self.dram_in("state", [NSMP, RH, RDK, RDV])
        ptab = self.dram_in("ptab", [NSMP, npages], I32)
        c_ident = self.dram_in("c_ident", [P, P])
        c_cosR = self.dram_in("c_cosR", [NTT, P, 64])
        c_sinR = self.dram_in("c_sinR", [NTT, P, 64])
        c_cosM = self.dram_in("c_cosM", [NTT, P, 16])
        c_sinM = self.dram_in("c_sinM", [NTT, P, 16])
        c_maskT = self.dram_in("c_maskT", [P, RH, P])
        c_qdec = self.dram_in("c_qdec", [P, RH, P])
        c_kdec = self.dram_in("c_kdec", [P, 8])
        c_tri = self.dram_in("c_tri", [P, P])
        c_onehot = self.dram_in("c_onehot", [P, NSMP])
        c_bmask = self.dram_in("c_bmask", [32, 512])
        c_sel2 = self.dram_in("c_sel2", [32, NSIDE])
        c_smask = self.dram_in("c_smask", [8, NSMP * NSIDE])
        y = self.dram_out("y", [nch * CH, D])
        ys = self.dram_out("ys", [NSMP, D])
        ckv_p = self.dram_out("ckv_p", [T, KVR])
        kr_p = self.dram_out("kr_p", [T, ROPE])
        S_p = self.dram_out("S_p", [RH, RDK, RDV])
        ckv_s = self.dram_out("ckv_s", [NSMP, KVR])
        kr_s = self.dram_out("kr_s", [NSMP, ROPE])
        S_s = self.dram_out("S_s", [NSMP, RH, RDK, RDV])

        self.init_psum()
        self.init_wring()

        def bfv(ps):
            return ps[:, :].bitcast(BF16)

        hT = self.sb("hT", [P, KD, CH], F32); hT_b = B("hT")
        uT = self.sb("uT", [P, KD, CH], BF16); uT_b = B("uT")
        rstd = self.sb("rstd", [P, CH], F32); rstd_b = B("rstd")
        ckvT_all = self.sb("ckvT_all", [P, 2, T], BF16); ckvT_b = B("ckvT")
        krT_b = B("krT")
        Vt = self.sb("Vt", [P, NTT, HM, VD + 1], BF16); V_b = B("V")
        kTh = self.sb("kTh", [96, T], BF16); kTh_b = B("kTh")
        S = self.sb("S", [P, RH, RDV], F32); S_bf = self.sb("S_bf", [P, RH, RDV], BF16); S_b = B("S"); Sbf_b = B("Sbf")
        w_uk_sb = self.sb("w_uk_sb", [P, 2, HM * NOPE], BF16)
        w_uv_sb = self.sb("w_uv_sb", [P, 2, HM * VD], BF16)
        w_ukT = self.sb("w_ukT", [NOPE, HM, KVR], BF16)
        wres_b = B("wres")
        cosR = self.sb("cosR", [P, 4, 64], F32); sinR = self.sb("sinR", [P, 4, 64], F32)
        cosM = self.sb("cosM", [P, 4, 16], F32); sinM = self.sb("sinM", [P, 4, 16], F32)
        rope_b = B("rope")
        maskT = self.sb("maskT", [P, RH, P], F32)
        qdec = self.sb("qdec", [P, RH, P], F32)
        kdec = self.sb("kdec", [P, 8], F32)
        tri_f = self.sb("tri_f", [P, P], F32)
        tri = self.sb("tri", [P, P], BF16)
        onehot = self.sb("onehot", [P, NSMP], F32)
        bmask = self.sb("bmask", [32, 512], F32)
        sel2_f = self.sb("sel2_f", [32, NSIDE], F32)
        sel2 = self.sb("sel2", [32, NSIDE], BF16)
        smask = self.sb("smask", [8, NSMP * NSIDE], F32)
        gains = self.sb("gains", [P, 4, KD], F32)
        qn_bc = self.sb("qn_bc", [P, QR], F32)
        kvn_bc = self.sb("kvn_bc", [P, KVR], F32)
        gn_bc = self.sb("gn_bc", [P, RH * RDV], F32)
        ident_f = self.sb("ident_f", [P, P], F32)
        ident_b = self.sb("ident_b", [P, P], BF16)
        onesD = self.sb("onesD", [P, P], BF16)
        ones_c = self.sb("ones_c", [P, 2], BF16)
        qropeT = self.sb("qropeT", [ROPE, HM, NSIDE], BF16)
        epsb = self.sb("epsb", [P, 1], F32)
        const_b = B("const")
        stat = self.sb("stat", [P, 32], F32)
        stat_b = B("stat")
        stat2 = self.sb("stat2", [P, 32], F32)
        stat2_b = B("stat2")
        stat3 = self.sb("stat3", [P, 4], F32)
        stat3_b = B("stat3")
        stat4 = self.sb("stat4", [P, 8], F32)
        tmr_b = [B(f"tmr{i}") for i in range(4)]
        NLS = npages // 4 + 1
        lsum = self.sb("lsum", [P, NSMP * NLS], F32)
        lsum_b = [B(f"lsum{i}") for i in range(NSMP)]
        pidx_sb = self.sb("pidx_sb", [P, NSMP * npages], I32)
        iota_sb = self.sb("iota_sb", [P, 1], F32)
        ptab_b = B("ptab")
        c_iota = self.dram_in("c_iota", [P, 1])
        ckv_rows = cache_ckv.rearrange("n (q u) c -> (n q) (u c)", u=4)
        kr_rows = cache_kr.rearrange("n (q u) c -> (n q) (u c)", u=4)

        USZ = 88 * 1024
        U = self.sb("U", [P, USZ // 2], BF16)

        def uv(off_bytes, nelem, dt, shape_str=None, **kw):
            esz = 2 if dt == BF16 else 4
            assert off_bytes % 4 == 0
            v = U[:, off_bytes // 2: off_bytes // 2 + nelem * esz // 2]
            if dt != BF16:
                v = v.bitcast(dt)
            if shape_str:
                v = v.rearrange(shape_str, **kw)
            return v

        hid = uv(0, KF * CH, BF16, "p (k n) -> p k n", k=KF); hid_b = B("hid")
        sq = uv(22528, KD * CH, BF16, "p (k n) -> p k n", k=KD); sq_kb = [B(f"sq{k_}") for k_ in range(KD)]
        sgt = [uv(30720 + i * 1024, CH, BF16) for i in range(2)]; sgt_b = [B("sgt0"), B("sgt1")]
        xin = [uv(i * 4096, D, F32) for i in range(4)]
        yout = [uv(22528 + i * 4096, D, F32) for i in range(2)]
        ZS = 3072

        def zf(slot, o, n):
            return uv(slot * ZS * 2 + o * 2, n, BF16)
        rq_tm = [zf(s_, 0, 512) for s_ in range(4)]
        rk_tm = [zf(s_, 512, 512) for s_ in range(4)]
        rv_tm = [zf(s_, 1024, 1024) for s_ in range(4)]
        rgg = [zf(s_, 2048, 1024) for s_ in range(4)]
        ga_s = [zf(s_, 0, 1024) for s_ in range(4)]
        gb_s = rv_tm
        m_tm = rgg
        zq_b = B("zq"); zv_b = B("zv"); zg_b = B("zg")
        o = 4 * ZS * 2
        qT = uv(o, HM * CH, BF16, "p (h n) -> p h n", h=HM); qT_b = B("qT"); o += 8192
        mT = qT; mT_b = qT_b
        cqnT = uv(o, 3 * CH, BF16, "p (k n) -> p k n", k=3); cqnT_b = B("cqnT"); o += 3072
        aT = uv(o, 4 * CH, BF16, "p (k n) -> p k n", k=4); aT_b = B("aT"); o += 4096
        rinT = uv(o, KD * CH, BF16, "p (k n) -> p k n", k=KD); rinT_b = B("rinT"); o += 8192
        q_tm = uv(o, 4 * HM * HD, BF16, "p (s h c) -> p s h c", s=4, h=HM); q_tm_b = B("q_tm"); o += 6144
        a_tm = uv(o - 6144, 4 * 512, BF16, "p (s c) -> p s c", s=4); a_tm_b = q_tm_b
        cqn_tms = [uv(o + i * 768, QR, BF16) for i in range(2)]; cqn_tms_b = [B("cqn_tm0"), B("cqn_tm1")]; o += 1536
        ckv_tm = [uv(o + i * 1024, KVR, F32) for i in range(2)]; ckv_tm_b = [B("ckv_tm0"), B("ckv_tm1")]; o += 2048
        kr_tm = [uv(o + i * 128, ROPE, F32) for i in range(2)]; kr_tm_b = [B("kr_tm0"), B("kr_tm1")]; o += 256
        krpad = uv(o, 96, BF16); krpad_b = B("krpad"); o += 192
        ropeA = uv(o, 256, F32, "p (h c) -> p h c", h=4); o += 1024
        ropeB = uv(o, 256, F32, "p (h c) -> p h c", h=4); o += 1024
        rope_t_b = B("ropeT")
        NPT = 4
        pTs = [uv(o + i * 1024, CH, BF16) for i in range(NPT)]; pT_b = [B(f"pT{i}") for i in range(NPT)]; o += NPT * 1024
        r_in = uv(o, 1024, BF16); r_in_b = B("r_in"); o += 2048
        ma = uv(o, 512, F32); ma_b = B("ma"); o += 2048
        cens = [uv(o + i * 4096, 1024, F32, "p (h e) -> p h e", h=4) for i in range(2)]; cens_b = [B("cen0"), B("cen1")]; o += 8192
        junk = ma; junk_b = ma_b
        rt = {}
        for nm_ in ("qT_sb", "qdT", "kT_sb", "sTm", "kd"):
            rt[nm_] = uv(o, RH * P, BF16); o += RH * P * 2
        ret_b = {k_: B(k_) for k_ in rt}
        Sp = uv(o, RDV, F32); o += 1024
        Sn = uv(o, RDV, F32); o += 1024
        Snb = uv(o, RDV, BF16); o += 512
        vm = uv(o, RDV, BF16); o += 512
        sret_b = B("sret")
        qTm = uv(o, RH * NSMP * NSIDE, BF16, "p (h s j) -> p h s j", h=RH, s=NSMP); qTm_b = B("qTm"); o += 640
        ckvTs = uv(o, 2 * NSIDE, BF16, "p (k j) -> p k j", k=2); o += 80
        krTs = uv(o, NSIDE + 4, BF16); o += 48
        qlatT = uv(o, 2 * HM * NSIDE, BF16, "p (k h j) -> p k h j", k=2, h=HM); o += 640
        ckvs_bf = uv(o, 258, BF16); o += 516 + 4
        o = (o + 3) // 4 * 4
        side_b = B("sidemisc")
        olT = uv(o, 2 * 32, BF16, "p (k j) -> p k j", k=2); o += 128
        am = uv(o, 512, BF16); o += 1024
        p20 = uv(o, NSIDE, F32); o += 80
        p20b = uv(o, NSIDE, BF16); o += 40
        pT20 = uv(o, 8, BF16); o += 16
        ol_sb = uv(o, KVR, BF16); o += 512
        smp_b = B("smpmisc")
        assert o <= USZ, o
        self.u_used = o
        so = ZS * 2
        NATW = 290
        NNAT = 4
        natc = [uv(so + i * 2304, 4 * KVR, BF16, "p (g c) -> p g c", g=4) for i in range(NNAT)]
        natk = [uv(so + i * 2304 + 2048, 4 * ROPE, BF16, "p (g c) -> p g c", g=4) for i in range(NNAT)]
        so += NNAT * 2304
        NSET = 2
        ckvT_sb, krT_sb, p_sb, pT_sb, sblk_b = [], [], [], [], []
        for i in range(NSET):
            ckvT_sb.append(uv(so, 1024, BF16)); so += 2048
            krT_sb.append(uv(so, 512, BF16)); so += 1024
            p_sb.append(uv(so, 512, BF16)); so += 1024
            pT_sb.append(uv(so, 32, BF16)); so += 64
            sblk_b.append({k_: B(k_ + str(i)) for k_ in ("ckvT_sb", "krT_sb", "p_sb", "pT_sb")})
        assert so <= 4 * ZS * 2, so
        natb_b = [B(f"natb{i}") for i in range(NNAT)]

        def cload(dst, src, **kw):
            self.dma(dst, src, [], [const_b], "const", **kw)
        cload(ident_f[:, :], c_ident)
        nblk_ = npages // 4
        for j in range(4):
            self.dma(pidx_sb[32 * j:32 * (j + 1), 0:NSMP * nblk_],
                     ptab.rearrange("s (b j) -> j (s b)", j=4)[j].partition_broadcast(32), [], [ptab_b], "ptab",
                     allow_slow_non_contiguous=True)
        self.dma(iota_sb[:, :], c_iota, [], [ptab_b], "ptab")
        idf = uv(0, NSMP * nblk_, F32)
        self.copy("dve", idf, pidx_sb[:, 0:NSMP * nblk_], [ptab_b], [ptab_b, hid_b])
        self.ts("dve", idf, idf, 32.0, iota_sb[:, 0:1], ALU.mult, ALU.add, [ptab_b], [ptab_b, hid_b])
        self.copy("dve", pidx_sb[:, 0:NSMP * nblk_], idf, [ptab_b], [ptab_b, hid_b])
        cload(maskT[:, :, :], c_maskT)
        cload(qdec[:, :, :], c_qdec)
        cload(kdec[:, :], c_kdec)
        cload(tri_f[:, :], c_tri)
        cload(onehot[:, :], c_onehot)
        cload(bmask[:, :], c_bmask)
        cload(sel2_f[:, :], c_sel2)
        cload(smask[:, :], c_smask)
        cload(qn_bc[:, :], q_norm.partition_broadcast(P))
        cload(kvn_bc[:, :], kv_norm.partition_broadcast(P))
        cload(gn_bc[:, :], ret_gn.partition_broadcast(P))
        for gi, g in enumerate([W["n1"], W["mixn"], W["n2"], W["fn"]]):
            cload(gains[:, gi, :], g.rearrange("(k p) -> p k", p=P), allow_slow_non_contiguous=True)
        self.copy("dve", ident_b[:, :], ident_f[:, :], [const_b], [const_b])
        self.copy("dve", tri[:, :], tri_f[:, :], [const_b], [const_b])
        self.copy("dve", sel2[:, :], sel2_f[:, :], [const_b], [const_b])
        self.memset("pool", onesD[:, :], 1.0 / D, [], [const_b])
        self.memset("pool", epsb[:, :], EPS, [], [const_b])
        self.memset("pool", ones_c[:, :], 1.0, [], [const_b])
        self.memset("pool", Vt[:, :, :, VD:VD + 1], 1.0, [], [V_b])
        self.memset("pool", stat[:, :], 0.0, [], [stat_b])
        wv_, wvb_ = self.load_w(w_uk.rearrange("(k p) c -> p k c", p=P), 2, HM * NOPE)
        self.copy("pool", w_uk_sb[:, :, :], wv_, [wvb_], [wres_b])
        wv_, wvb_ = self.load_w(w_uv.rearrange("(k p) c -> p k c", p=P), 2, HM * VD)
        self.copy("pool", w_uv_sb[:, :, :], wv_, [wvb_], [wres_b])
        for hh in range(2):
            ps, psb = self.bank()
            v = bfv(ps)
            for h4 in range(4):
                h = hh * 4 + h4
                for rk in range(2):
                    self.tr(v[0:NOPE, (h4 * 2 + rk) * P:(h4 * 2 + rk + 1) * P], w_uk_sb[:, rk, h * NOPE:(h + 1) * NOPE],
                            ident_b[:, :], [wres_b, const_b], [psb])
            self.copy("dve", w_ukT[:, hh * 4:(hh + 1) * 4, :],
                      v[0:NOPE, 0:1024].rearrange("p (h c) -> p h c", h=4), [psb], [wres_b])

        tgl = [0]

        def evac_copy(out, in_, reads, writes):
            tgl[0] += 1
            return self.copy("act" if tgl[0] % 2 else "dve", out, in_, reads, writes)

        class _Stop(Exception):
            pass
        stop_at = self.cfg.get("stop_at")

        def chk(name):
            if stop_at == name:
                raise _Stop()

        class Tile:
            pass

        def mk_pass(kind, c):
            p_ = Tile()
            p_.kind = kind
            p_.c = c
            p_.tiles = []
            if kind == "side":
                p_.n = NSIDE
                t = Tile(); t.slot = 0; t.gt = 0; t.rows = NSIDE; t.c0 = 0; t.g0 = 0; t.gn = NMETA
                p_.tiles.append(t)
            else:
                p_.n = CH
                for tt in range(4):
                    t = Tile(); t.slot = tt; t.gt = 1 + c * 4 + tt; t.rows = P; t.c0 = tt * P
                    t.g0 = NMETA + (c * 4 + tt) * P; t.gn = P
                    p_.tiles.append(t)
            return p_

        def rms_stats(n):
            ps, psb = self.bank()
            for k in range(KD):
                if k % 2 == 0:
                    self.activation(sq[:, k, 0:n], hT[:, k, 0:n], AF.Square, [hT_b], [sq_kb[k]])
                else:
                    self.tt("dve", sq[:, k, 0:n], hT[:, k, 0:n], hT[:, k, 0:n], ALU.mult, [hT_b], [sq_kb[k]])
                self.mm(ps[:, 0:n], onesD[:, :], sq[:, k, 0:n], k == 0, k == KD - 1, [sq_kb[k], const_b], [psb])
            self.activation(rstd[:, 0:n], ps[:, 0:n], AF.Sqrt, [psb, const_b], [rstd_b], bias=epsb[:, 0:1], scale=1.0)
            self.recip(rstd[:, 0:n], rstd[:, 0:n], [rstd_b], [rstd_b])

        def rmsnorm_fm(n, gi):
            rms_stats(n)
            for k in range(KD):
                self.stt("dve", uT[:, k, 0:n], hT[:, k, 0:n], gains[:, gi, k:k + 1], rstd[:, 0:n], ALU.mult, ALU.mult,
                         [hT_b, rstd_b, const_b], [uT_b])

        def ffn(n, gi, Wg, Wu, Wd):
            rmsnorm_fm(n, gi)
            FG = 2
            for jg in range(KF // FG):
                wg, wgb = self.load_w(Wg[:, jg * FG * P:(jg + 1) * FG * P].rearrange("(k p) f -> p k f", p=P), KD, FG * P)
                wu, wub = self.load_w(Wu[:, jg * FG * P:(jg + 1) * FG * P].rearrange("(k p) f -> p k f", p=P), KD, FG * P)
                for jj in range(FG):
                    j = jg * FG + jj
                    psg, psgb = self.bank()
                    psu, psub = self.bank()
                    for k in range(KD):
                        self.mm(psg[:, 0:n], wg[:, k, jj * P:(jj + 1) * P], uT[:, k, 0:n], k == 0, k == KD - 1, [wgb, uT_b], [psgb])
                    for k in range(KD):
                        self.mm(psu[:, 0:n], wu[:, k, jj * P:(jj + 1) * P], uT[:, k, 0:n], k == 0, k == KD - 1, [wub, uT_b], [psub])
                    si = j % 2
                    self.activation(sgt[si][:, 0:n], psg[:, 0:n], AF.Silu, [psgb], [sgt_b[si]])
                    self.tt("dve", hid[:, j, 0:n], sgt[si][:, 0:n], psu[:, 0:n], ALU.mult, [sgt_b[si], psub], [hid_b])
            dbanks = [self.rbank(i) for i in range(KD)]
            for kg in range(KF // 2):
                w, wbuf = self.load_w(Wd[kg * 2 * P:(kg + 1) * 2 * P, :].rearrange("(k p) d -> p k d", p=P), 2, D)
                for kk in range(2):
                    k = kg * 2 + kk
                    for i in range(KD):
                        self.mm(dbanks[i][0][:, 0:n], w[:, kk, i * P:(i + 1) * P], hid[:, k, 0:n], k == 0, k == KF - 1,
                                [wbuf, hid_b], [dbanks[i][1]])
            for i in range(KD):
                self.stt("dve", hT[:, i, 0:n], dbanks[i][0][:, 0:n], 0.5, hT[:, i, 0:n], ALU.mult, ALU.add, [dbanks[i][1], hT_b], [hT_b])

        def load_x_dma(pas):
            if pas.kind == "side":
                self.dma(xin[0][0:NMETA, :], meta, [], [hid_b], "xin")
                self.dma(xin[0][NMETA:NSIDE, :], xs, [], [hid_b], "xin")
            else:
                for tt in range(4):
                    r0 = (pas.c * 4 + tt) * P
                    self.dma(xin[tt][:, :], x[r0:r0 + P, :], [], [hid_b], "xin")

        def load_x(pas):
            if pas.kind == "side":
                for k in range(KD):
                    ps, psb = self.bank()
                    self.tr(ps[:, 0:NSIDE], xin[0][0:NSIDE, k * P:(k + 1) * P], ident_f[0:NSIDE, 0:NSIDE], [hid_b, const_b], [psb])
                    evac_copy(hT[:, k, 0:NSIDE], ps[:, 0:NSIDE], [psb], [hT_b])
            else:
                for k in range(KD):
                    ps, psb = self.bank()
                    for tt in range(4):
                        self.tr(ps[:, tt * P:(tt + 1) * P], xin[tt][:, k * P:(k + 1) * P], ident_f[:, :], [hid_b, const_b], [psb])
                    evac_copy(hT[:, k, :], ps[:, :], [psb], [hT_b])

        def final_norm_out(pas):
            n = pas.n
            rms_stats(n)
            for k in range(KD):
                self.stt("dve", hT[:, k, 0:n], hT[:, k, 0:n], gains[:, 3, k:k + 1], rstd[:, 0:n], ALU.mult, ALU.mult,
                         [hT_b, rstd_b, const_b], [hT_b])
            if pas.kind == "main":
                for tt in range(4):
                    yo = yout[tt % 2]
                    for kk in range(2):
                        ps, psb = self.bank()
                        for k4 in range(4):
                            k = kk * 4 + k4
                            self.tr(ps[:, k4 * P:(k4 + 1) * P], hT[:, k, tt * P:(tt + 1) * P], ident_f[:, :], [hT_b, const_b], [psb])
                        evac_copy(yo[:, kk * 512:(kk + 1) * 512], ps[:, :], [psb], sq_kb[(tt % 2) * 4:(tt % 2) * 4 + 4])
                    r0 = (pas.c * 4 + tt) * P
                    self.dma(y[r0:r0 + P, :], yo[:, :], sq_kb[(tt % 2) * 4:(tt % 2) * 4 + 4], [], f"o_y{tt % 2}")
            else:
                yo = yout[0]
                for kk in range(2):
                    ps, psb = self.bank()
                    for k4 in range(4):
                        k = kk * 4 + k4
                        self.mm(ps[0:NSIDE, k4 * P:(k4 + 1) * P], hT[:, k, 0:NSIDE], ident_f[:, :], True, True, [hT_b, const_b], [psb])
                    evac_copy(yo[0:NSIDE, kk * 512:(kk + 1) * 512], ps[0:NSIDE, :], [psb], sq_kb[0:4])
                self.dma(ys, yo[NMETA:NSIDE, :], sq_kb[0:4], [], "o_y0")

        hi_tgl = [0]

        def proj_tm(pas, srcT, srcb, KK, Wsrc, col0, ncols, evac, allow_hi=False):
            klen = min(KK, self.WSLOT // ncols)
            use_hi = False
            if allow_hi:
                hi_tgl[0] += 1
                use_hi = hi_tgl[0] % 2 == 1
            if use_hi:
                banks = [self.rbank(4 + i) for i in range(len(pas.tiles))]
            else:
                banks = [self.bank() for _ in pas.tiles]
            k0 = 0
            while k0 < KK:
                kl = min(klen, KK - k0)
                w, wbuf = self.load_w(Wsrc[k0 * P:(k0 + kl) * P, col0:col0 + ncols].rearrange("(k p) c -> p k c", p=P), kl, ncols)
                for ti, t in enumerate(pas.tiles):
                    ps, psb = banks[ti]
                    for kk in range(kl):
                        k = k0 + kk
                        self.mm(ps[0:t.rows, 0:ncols], srcT[:, k, t.c0:t.c0 + t.rows], w[:, kk, :], k == 0, k == KK - 1,
                                [srcb, wbuf], [psb])
                k0 += kl
            for ti, t in enumerate(pas.tiles):
                evac(t, banks[ti][0], banks[ti][1])

        def tm_rms(ps, psb, r, c0, ncols, gbc, out, outb, slot):
            sb_ = tmr_b[slot]
            c_ = slot * 2
            self.memset("dve", stat4[0:r, c_:c_ + 1], 0.0, [], [sb_])
            self.activation(junk[0:r, 0:ncols], ps[0:r, c0:c0 + ncols], AF.Square, [psb, sb_], [sb_],
                            accum_out=stat4[0:r, c_:c_ + 1])
            self.activation(stat4[0:r, c_ + 1:c_ + 2], stat4[0:r, c_:c_ + 1], AF.Sqrt, [sb_, const_b], [sb_],
                            bias=epsb[0:r, 0:1], scale=1.0 / ncols)
            self.recip(stat4[0:r, c_ + 1:c_ + 2], stat4[0:r, c_ + 1:c_ + 2], [sb_], [sb_])
            self.stt("dve", out, ps[0:r, c0:c0 + ncols], stat4[0:r, c_ + 1:c_ + 2], gbc[0:r, :], ALU.mult, ALU.mult,
                     [psb, sb_, const_b], [outb])

        def rope_tm(x1, x2, cos, sin, o1, o2, r, nh, w, sc, reads, wb):
            A = ropeA[0:r, 0:nh, 0:w]
            Bt = ropeB[0:r, 0:nh, 0:w]
            rd = list(reads) + [rope_b]
            self.stt("dve", A, x1, sc, cos, ALU.mult, ALU.mult, rd, [rope_t_b])
            self.stt("dve", Bt, x2, sc, sin, ALU.mult, ALU.mult, rd, [rope_t_b])
            self.tt("dve", o1, A, Bt, ALU.subtract, [rope_t_b], [wb])
            self.stt("dve", A, x1, sc, sin, ALU.mult, ALU.mult, rd + [wb], [rope_t_b])
            self.stt("dve", Bt, x2, sc, cos, ALU.mult, ALU.mult, rd, [rope_t_b])
            self.tt("dve", o2, A, Bt, ALU.add, [rope_t_b], [wb])

        def bc(ap, r, nh, w):
            return ap.unsqueeze(1).broadcast_to([r, nh, w])

        def mixer_proj(pas):
            n = pas.n
            rmsnorm_fm(n, 1)
            g0t = pas.tiles[0].gt
            nt = len(pas.tiles)
            for dst, src in ((cosR, c_cosR), (sinR, c_sinR), (cosM, c_cosM), (sinM, c_sinM)):
                self.dma(dst[:, 0:nt, :], src[g0t:g0t + nt].rearrange("t p c -> p t c"), [], [rope_b], "rope")

            def ev_cq(t, ps, psb):
                r = t.rows
                cqn_tm, cqn_tm_b = cqn_tms[t.slot % 2], cqn_tms_b[t.slot % 2]
                tm_rms(ps, psb, r, 0, QR, qn_bc, cqn_tm[0:r, :], cqn_tm_b, t.slot)
                pt, ptb = self.bank2()
                v = bfv(pt)
                for kq in range(3):
                    self.tr(v[:, kq * P:kq * P + r], cqn_tm[0:r, kq * P:(kq + 1) * P], ident_b[0:r, 0:r], [cqn_tm_b, const_b], [ptb])
                self.copy("act", cqnT[:, :, t.c0:t.c0 + r], v[:, 0:3 * P].rearrange("p (k c) -> p k c", k=3)[:, :, 0:r], [ptb], [cqnT_b])
            chk("p:a")
            proj_tm(pas, uT, uT_b, KD, w_in, OFF_CQ, QR, ev_cq)
            chk("p:cq")

            for half in range(2):
                def ev_q(t, ps, psb, half=half):
                    r = t.rows
                    psv = ps[0:r, 0:4 * HD].rearrange("p (h c) -> p h c", h=4)
                    dst = q_tm[0:r, t.slot, half * 4:(half + 1) * 4, :]
                    sk = self.cfg.get("dbgskip", "")
                    if "nope" not in sk:
                        self.copy("act", dst[:, :, 0:NOPE], psv[:, :, 0:NOPE], [psb], [q_tm_b])
                    if "rope" not in sk:
                        rope_tm(psv[:, :, 64:80], psv[:, :, 80:96], bc(cosM[0:r, t.slot, :], r, 4, 16), bc(sinM[0:r, t.slot, :], r, 4, 16),
                                dst[:, :, 64:80], dst[:, :, 80:96], r, 4, 16, 1.0, [psb], q_tm_b)
                proj_tm(pas, cqnT, cqnT_b, 3, w_uq, half * 4 * HD, 4 * HD, ev_q)
            chk("p:q")
            for h in range(HM):
                pt, ptb = self.bank()
                v = bfv(pt)
                for t in pas.tiles:
                    self.tr(v[0:HD, t.c0:t.c0 + t.rows], q_tm[0:t.rows, t.slot, h, :], ident_b[0:t.rows, 0:t.rows], [q_tm_b, const_b], [ptb])
                evac_copy(qT[0:HD, h, 0:n], v[0:HD, 0:n], [ptb], [qT_b])

            chk("p:qT")

            def ev_ckvkr(t, ps, psb):
                r = t.rows
                i2 = t.gt % 2
                ck = ckv_tm[i2]; ckb = ckv_tm_b[i2]
                tm_rms(ps, psb, r, 0, KVR, kvn_bc, ck[0:r, :], ckb, t.slot)
                if pas.kind == "main":
                    self.dma(ckv_p[t.g0:t.g0 + P, :], ck[:, :], [ckb], [], f"o_ckv{i2}")
                else:
                    self.dma(ckv_p[0:NMETA, :], ck[0:NMETA, :], [ckb], [], f"o_ckv{i2}")
                    self.dma(ckv_s, ck[NMETA:NSIDE, :], [ckb], [], f"o_ckv{i2}")
                pt, ptb = self.bank2()
                for rk in range(2):
                    self.tr(pt[:, rk * P:rk * P + r], ck[0:r, rk * P:(rk + 1) * P], ident_f[0:r, 0:r], [ckb, const_b], [ptb])
                self.copy("act", ckvT_all[:, :, t.g0:t.g0 + t.gn], pt[:, 0:2 * P].rearrange("p (k c) -> p k c", k=2)[:, :, 0:t.gn],
                          [ptb], [ckvT_b])
                if pas.kind == "side":
                    self.copy("act", ckvTs[:, :, :], pt[:, 0:2 * P].rearrange("p (k c) -> p k c", k=2)[:, :, 0:NSIDE], [ptb], [side_b])
                    self.memset("pool", ckvs_bf[0:NSIDE, 0:1], 1.0, [], [side_b])
                    self.memset("pool", ckvs_bf[0:NSIDE, 1:2], 0.0, [], [side_b])
                    self.copy("dve", ckvs_bf[0:NSIDE, 2:258], ck[0:NSIDE, :], [ckb], [side_b])
                kr = kr_tm[i2]; krb = kr_tm_b[i2]
                A = ropeA[0:r, 0, 0:16]; Bt = ropeB[0:r, 0, 0:16]
                c_, s_ = cosM[0:r, t.slot, :], sinM[0:r, t.slot, :]
                x1, x2 = ps[0:r, 256:272], ps[0:r, 272:288]
                self.tt("dve", A, x1, c_, ALU.mult, [psb, rope_b], [rope_t_b])
                self.tt("dve", Bt, x2, s_, ALU.mult, [psb, rope_b], [rope_t_b])
                self.tt("dve", kr[0:r, 0:16], A, Bt, ALU.subtract, [rope_t_b], [krb])
                self.tt("dve", A, x1, s_, ALU.mult, [psb, rope_b, krb], [rope_t_b])
                self.tt("dve", Bt, x2, c_, ALU.mult, [psb, rope_b], [rope_t_b])
                self.tt("dve", kr[0:r, 16:32], A, Bt, ALU.add, [rope_t_b], [krb])
                if pas.kind == "main":
                    self.dma(kr_p[t.g0:t.g0 + P, :], kr[:, :], [krb], [], f"o_kr{i2}")
                else:
                    self.dma(kr_p[0:NMETA, :], kr[0:NMETA, :], [krb], [], f"o_kr{i2}")
                    self.dma(kr_s, kr[NMETA:NSIDE, :], [krb], [], f"o_kr{i2}")
                self.copy("dve", krpad[0:r, 64:96], kr[0:r, :], [krb], [krpad_b])
                pt2, pt2b = self.bank2()
                v2 = bfv(pt2)
                self.tr(v2[0:96, 0:r], krpad[0:r, 0:96], ident_b[0:r, 0:r], [krpad_b, const_b], [pt2b])
                self.copy("act", kTh[64:96, t.g0:t.g0 + t.gn], v2[64:96, 0:t.gn], [pt2b], [krT_b])
                if pas.kind == "side":
                    self.copy("act", krTs[64:96, 0:NSIDE], v2[64:96, 0:NSIDE], [pt2b], [side_b])
                pv, pvb = self.bank2()
                for rk in range(2):
                    self.mm(pv[0:t.gn, 0:512], ckvT_all[:, rk, t.g0:t.g0 + t.gn], w_uv_sb[:, rk, :], rk == 0, rk == 1,
                            [ckvT_b, wres_b], [pvb])
                self.copy("dve", Vt[0:t.gn, t.gt, :, 0:VD], pv[0:t.gn, 0:512].rearrange("p (h v) -> p h v", h=HM), [pvb], [V_b])
            self.memset("pool", krpad[:, 0:64], 0.0, [], [krpad_b])
            proj_tm(pas, uT, uT_b, KD, w_in, OFF_CKV, KVR + ROPE, ev_ckvkr)

            chk("p:ckv")

            def ev_rqk(dst_list, sc):
                def ev(t, ps, psb):
                    r = t.rows
                    psv = ps[0:r, 0:512].rearrange("p (h c) -> p h c", h=RH)
                    dst = dst_list[t.slot][0:r, :].rearrange("p (h c) -> p h c", h=RH)
                    rope_tm(psv[:, :, 0:64], psv[:, :, 64:128], bc(cosR[0:r, t.slot, :], r, RH, 64), bc(sinR[0:r, t.slot, :], r, RH, 64),
                            dst[:, :, 0:64], dst[:, :, 64:128], r, RH, 64, sc, [psb], zq_b)
                return ev
            proj_tm(pas, uT, uT_b, KD, w_in, OFF_RQ, 512, ev_rqk(rq_tm, 1.0), allow_hi=True)
            proj_tm(pas, uT, uT_b, KD, w_in, OFF_RK, 512, ev_rqk(rk_tm, float(RDK) ** -0.5), allow_hi=True)
            chk("p:rqk")
            for half in range(2):
                def ev_rv(t, ps, psb, half=half):
                    evac_copy(rv_tm[t.slot][0:t.rows, half * 512:(half + 1) * 512], ps[0:t.rows, 0:512], [psb], [zv_b])
                proj_tm(pas, uT, uT_b, KD, w_in, OFF_RV + half * 512, 512, ev_rv, allow_hi=True)
            for half in range(2):
                def ev_rg(t, ps, psb, half=half):
                    r = t.rows
                    self.activation(ma[0:r, :], ps[0:r, 0:512], AF.Silu, [psb], [ma_b])
                    self.tt("dve", rgg[t.slot][0:r, half * 512:(half + 1) * 512], ma[0:r, :], gn_bc[0:r, half * 512:(half + 1) * 512],
                            ALU.mult, [ma_b, const_b], [zg_b])
                proj_tm(pas, uT, uT_b, KD, w_in, OFF_RG + half * 512, 512, ev_rg, allow_hi=True)

        def mla_main(pas):
            c = pas.c
            Tk = NMETA + CH * (c + 1)
            nkt = 1 + 4 * (c + 1)
            for h in range(HM):
                for g0c in range(0, Tk, 512):
                    ncol = min(512, Tk - g0c)
                    ps, psb = self.bank()
                    for rk in range(2):
                        self.mm(ps[0:NOPE, 0:ncol], w_uk_sb[:, rk, h * NOPE:(h + 1) * NOPE], ckvT_all[:, rk, g0c:g0c + ncol],
                                rk == 0, rk == 1, [wres_b, ckvT_b], [psb])
                    evac_copy(kTh[0:NOPE, g0c:g0c + ncol], ps[0:NOPE, 0:ncol], [psb], [kTh_b])
                yield
                pos_ = [self.rbank(4 + tq) for tq in range(4)]

                def geom(kt):
                    if kt == 0:
                        kc0, nk, lt = 0, NMETA, -1
                    else:
                        kc0, nk, lt = NMETA + (kt - 1) * P, P, kt - 1 - 4 * c
                    tq0 = max(0, lt)
                    return kc0, nk, lt, tq0

                def qk(kt):
                    kc0, nk, lt, tq0 = geom(kt)
                    q0 = tq0 * P
                    N = CH - q0
                    ps, psb = self.bank()
                    self.mm(ps[0:nk, 0:N], kTh[0:HD, kc0:kc0 + nk], qT[0:HD, h, q0:CH], True, True, [kTh_b, krT_b, qT_b], [psb])
                    pi = kt % NPT
                    pT = pTs[pi]
                    self.activation(pT[0:nk, 0:N], ps[0:nk, 0:N], AF.Exp, [psb], [pT_b[pi]], scale=MLA_SCALE)
                    if lt >= 0:
                        self.tt("dve", pT[:, 0:P], pT[:, 0:P], tri[:, :], ALU.mult, [pT_b[pi], const_b], [pT_b[pi]])

                def pv(kt):
                    kc0, nk, lt, tq0 = geom(kt)
                    pi = kt % NPT
                    pT = pTs[pi]
                    for tq in range(tq0, 4):
                        self.mm(pos_[tq][0][:, 0:VD + 1], pT[0:nk, (tq - tq0) * P:(tq - tq0 + 1) * P], Vt[0:nk, kt, h, :],
                                kt == 0, kt == 1 + 4 * c + tq, [pT_b[pi], V_b], [pos_[tq][1]])
                LAG = 3
                for kt in range(min(LAG, nkt)):
                    qk(kt)
                for kt in range(nkt):
                    if kt + LAG < nkt:
                        qk(kt + LAG)
                    pv(kt)
                    yield
                for tq in range(4):
                    self.recip(stat3[:, tq:tq + 1], pos_[tq][0][:, VD:VD + 1], [pos_[tq][1]], [stat3_b])
                    self.ts("dve", a_tm[:, tq, h * VD:(h + 1) * VD], pos_[tq][0][:, 0:VD], stat3[:, tq:tq + 1], None, ALU.mult, None,
                            [pos_[tq][1], stat3_b], [a_tm_b])
                yield

        def a_transpose(pas):
            for t in pas.tiles:
                r = t.rows
                pt, ptb = self.bank()
                v = bfv(pt)
                for k4 in range(4):
                    self.tr(v[:, k4 * P:k4 * P + r], a_tm[0:r, t.slot, k4 * P:(k4 + 1) * P], ident_b[0:r, 0:r], [a_tm_b, const_b], [ptb])
                evac_copy(aT[:, :, t.c0:t.c0 + r], v[:, 0:512].rearrange("p (k c) -> p k c", k=4)[:, :, 0:r], [ptb], [aT_b])

        lg = [float(np.log1p(-2.0 ** (-5.0 - h))) for h in range(RH)]

        def groupnorm_gate(t, ci):
            r = t.rows
            cen, cen_b = cens[ci], cens_b[ci]
            self.S.add("dve", lambda e: e.reduce_sum(out=stat[0:r, 0:4], in_=cen[0:r, :, :], axis=AX.X), [cen_b], [stat_b])
            self.ts("dve", stat[0:r, 4:8], stat[0:r, 0:4], -1.0 / RDV, None, ALU.mult, None, [stat_b], [stat_b])
            self.memset("pool", stat[0:r, 8:12], 0.0, [stat_b], [stat_b])
            for h in range(RH):
                self.activation(cen[0:r, h, :], cen[0:r, h, :], AF.Identity, [stat_b, cen_b], [cen_b], bias=stat[0:r, 4 + h:5 + h], scale=1.0)
            for h in range(RH):
                self.activation(junk[0:r, 0:RDV], cen[0:r, h, :], AF.Square, [cen_b, stat_b], [junk_b, stat_b],
                                accum_out=stat[0:r, 8 + h:9 + h])
            self.activation(stat[0:r, 12:16], stat[0:r, 8:12], AF.Sqrt, [stat_b, const_b], [stat_b], bias=epsb[0:r, 0:1], scale=1.0 / RDV)
            self.recip(stat[0:r, 12:16], stat[0:r, 12:16], [stat_b], [stat_b])
            for h in range(RH):
                self.stt("dve", r_in[0:r, h * RDV:(h + 1) * RDV], cen[0:r, h, :], stat[0:r, 12 + h:13 + h],
                         rgg[t.slot][0:r, h * RDV:(h + 1) * RDV], ALU.mult, ALU.mult, [cen_b, stat_b, zg_b], [r_in_b])
            pt, ptb = self.bank()
            v = bfv(pt)
            for k8 in range(KD):
                self.tr(v[:, k8 * P:k8 * P + r], r_in[0:r, k8 * P:(k8 + 1) * P], ident_b[0:r, 0:r], [r_in_b, const_b], [ptb])
            evac_copy(rinT[:, :, t.c0:t.c0 + r], v[:, 0:1024].rearrange("p (k c) -> p k c", k=KD)[:, :, 0:r], [ptb], [rinT_b])

        def retention_main(pas):
            W4 = RH * P
            for t in pas.tiles:
                sl = t.slot
                ci = t.slot % 2
                pt, ptb = self.bank()
                v = bfv(pt)
                for h in range(RH):
                    self.tr(v[:, h * P:(h + 1) * P], rq_tm[sl][:, h * P:(h + 1) * P], ident_b[:, :], [zq_b, const_b], [ptb])
                for h in range(RH):
                    self.tr(v[:, W4 + h * P:W4 + (h + 1) * P], rk_tm[sl][:, h * P:(h + 1) * P], ident_b[:, :], [zq_b, const_b], [ptb])
                self.copy("act", rt["qT_sb"][:, :], v[:, 0:W4], [ptb], [ret_b["qT_sb"]])
                self.tt("dve", rt["qdT"][:, :], v[:, 0:W4], qdec[:, :, :].rearrange("p h i -> p (h i)"), ALU.mult, [ptb, const_b], [ret_b["qdT"]])
                self.copy("act", rt["kT_sb"][:, :], v[:, W4:2 * W4], [ptb], [ret_b["kT_sb"]])
                for h in range(RH):
                    self.ts("dve", rt["kd"][:, h * P:(h + 1) * P], rk_tm[sl][:, h * P:(h + 1) * P], kdec[:, h:h + 1], None, ALU.mult, None,
                            [zq_b, const_b], [ret_b["kd"]])
                yield
                ps, psb = self.bank()
                for h in range(RH):
                    self.mm(ps[:, h * P:(h + 1) * P], rt["kT_sb"][:, h * P:(h + 1) * P], rt["qT_sb"][:, h * P:(h + 1) * P], True, True,
                            [ret_b["kT_sb"], ret_b["qT_sb"]], [psb])
                self.tt("dve", rt["sTm"][:, :], ps[:, 0:W4], maskT[:, :, :].rearrange("p h i -> p (h i)"), ALU.mult, [psb, const_b], [ret_b["sTm"]])
                yield
                for b2 in range(2):
                    po_, pob_ = self.bank()
                    for hh in range(2):
                        h = b2 * 2 + hh
                        self.mm(po_[:, hh * RDV:(hh + 1) * RDV], rt["sTm"][:, h * P:(h + 1) * P], rv_tm[sl][:, h * RDV:(h + 1) * RDV], True, False,
                                [ret_b["sTm"], zv_b], [pob_])
                        self.mm(po_[:, hh * RDV:(hh + 1) * RDV], rt["qdT"][:, h * P:(h + 1) * P], S_bf[:, h, :], False, True,
                                [ret_b["qdT"], Sbf_b], [pob_])
                    self.copy("act", cens[ci][:, b2 * 2:(b2 + 1) * 2, :], po_[:, 0:2 * RDV].rearrange("p (h e) -> p h e", h=2), [pob_], [cens_b[ci]])
                yield
                for b2 in range(2):
                    ps2, ps2b = self.bank()
                    for hh in range(2):
                        h = b2 * 2 + hh
                        self.mm(ps2[:, hh * RDV:(hh + 1) * RDV], rt["kd"][:, h * P:(h + 1) * P], rv_tm[sl][:, h * RDV:(h + 1) * RDV], True, True,
                                [ret_b["kd"], zv_b], [ps2b])
                    for hh in range(2):
                        h = b2 * 2 + hh
                        self.stt("dve", S[:, h, :], S[:, h, :], float(np.exp(lg[h] * P)), ps2[:, hh * RDV:(hh + 1) * RDV], ALU.mult, ALU.add,
                                 [S_b, ps2b], [S_b])
                self.copy("act", S_bf[:, :, :], S[:, :, :], [S_b], [Sbf_b])
                yield
                groupnorm_gate(t, ci)
                yield

        def retention_side(pas):
            t = pas.tiles[0]
            r = NSIDE
            self.memset("pool", qTm[:, :, :, :], 0.0, [], [qTm_b])
            for h in range(RH):
                pt, ptb = self.bank()
                v = bfv(pt)
                self.tr(v[:, 0:r], rq_tm[0][0:r, h * P:(h + 1) * P], ident_b[0:r, 0:r], [zq_b, const_b], [ptb])
                for s in range(NSMP):
                    self.copy("dve", qTm[:, h, s, NMETA + s:NMETA + s + 1], v[:, NMETA + s:NMETA + s + 1], [ptb], [qTm_b])
                self.ts("pool", rt["kd"][0:r, 0:P], rk_tm[0][0:r, h * P:(h + 1) * P], kdec[0:r, 4 + h:5 + h], None, ALU.mult, None,
                        [zq_b, const_b], [ret_b["kd"]])
                ps2, ps2b = self.bank()
                self.mm(ps2[:, 0:RDV], rt["kd"][0:r, 0:P], rv_tm[0][0:r, h * RDV:(h + 1) * RDV], True, True, [ret_b["kd"], zv_b], [ps2b])
                self.copy("dve", S[:, h, :], ps2[:, 0:RDV], [ps2b], [S_b])
                self.copy("act", S_bf[:, h, :], S[:, h, :], [S_b], [Sbf_b])
                yield
            for h in range(RH):
                for s in range(NSMP):
                    self.dma(Sp[:, :], state[s, h], [], [sret_b], "sp_in")
                    self.ts("dve", vm[0:r, :], rv_tm[0][0:r, h * RDV:(h + 1) * RDV], onehot[0:r, s:s + 1], None, ALU.mult, None,
                            [zv_b, const_b, sret_b], [sret_b])
                    ps, psb = self.bank()
                    self.mm(ps[:, 0:RDV], rk_tm[0][0:r, h * P:(h + 1) * P], vm[0:r, :], True, True, [zq_b, sret_b], [psb])
                    self.stt("dve", Sn[:, :], Sp[:, :], float(np.exp(lg[h])), ps[:, 0:RDV], ALU.mult, ALU.add, [sret_b, psb], [sret_b])
                    self.copy("act", Snb[:, :], Sn[:, :], [sret_b], [sret_b])
                    self.dma(S_s[s, h], Sn[:, :], [sret_b], [], "o_Ss")
                    po_, pob_ = self.bank()
                    self.mm(po_[0:r, 0:RDV], qTm[:, h, s, :], Snb[:, :], True, True, [qTm_b, sret_b], [pob_])
                    if s == 0:
                        self.copy("dve", cens[0][0:r, h, :], po_[0:r, 0:RDV], [pob_], [cens_b[0]])
                    else:
                        self.tt("dve", cens[0][0:r, h, :], po_[0:r, 0:RDV], cens[0][0:r, h, :], ALU.add, [pob_, cens_b[0]], [cens_b[0]])
                    yield
            groupnorm_gate(t, 0)
            yield

        def mla_samples(pas):
            regs = self.sp_regs
            for rk in range(2):
                ps, psb = self.bank()
                for h in range(HM):
                    self.mm(ps[:, h * NSIDE:(h + 1) * NSIDE], w_ukT[0:NOPE, h, rk * P:(rk + 1) * P], qT[0:NOPE, h, 0:NSIDE], True, True,
                            [wres_b, qT_b], [psb])
                self.copy("act", qlatT[:, rk, :, :], ps[:, 0:HM * NSIDE].rearrange("p (h j) -> p h j", h=HM), [psb], [smp_b])
            self.memset("dve", lsum[:, :], 0.0, [], lsum_b)
            pq, pqb = self.bank()
            vq = bfv(pq)
            for h in range(HM):
                self.tr(vq[0:ROPE, h * NSIDE:(h + 1) * NSIDE], q_tm[0:NSIDE, 0, h, NOPE:HD], ident_b[0:NSIDE, 0:NSIDE],
                        [q_tm_b, const_b], [pqb])
            self.copy("act", qropeT[:, :, :], vq[0:ROPE, 0:HM * NSIDE].rearrange("p (h j) -> p h j", h=HM), [pqb], [smp_b])
            nblk = npages // 4
            accs = [self.rbank(4 + s) for s in range(NSMP)]
            NB = nblk * NSMP
            blk = {}

            def st_g(i):
                b, s = divmod(i, NSMP)
                i2 = i % NNAT
                nc_, nk_ = natc[i2], natk[i2]
                icol = s * nblk + b

                def mk(dst, src_rows, icol=icol):
                    def fn(e):
                        return e.indirect_dma_start(out=dst, out_offset=None, in_=src_rows,
                                                    in_offset=bass.IndirectOffsetOnAxis(ap=pidx_sb[:, icol:icol + 1], axis=0))
                    return fn
                self.S.add("pool", mk(nc_[:, :, :].rearrange("p u c -> p (u c)"), ckv_rows), [ptab_b], [natb_b[i2]], dma_sem=self.dsem(f"natb{i2}"))
                self.S.add("pool", mk(nk_[:, :, :].rearrange("p u c -> p (u c)"), kr_rows), [ptab_b], [natb_b[i2]], dma_sem=self.dsem(f"natb{i2}"))

            def st_a(i):
                i2 = i % NNAT
                st_ = i % NSET
                nc_, nk_ = natc[i2], natk[i2]
                sb_ = sblk_b[st_]
                pa, pab = self.bank()
                va = bfv(pa)
                for pg in range(4):
                    for rk in range(2):
                        self.tr(va[:, rk * 512 + pg * P:rk * 512 + (pg + 1) * P], nc_[:, pg, rk * P:(rk + 1) * P], ident_b[:, :],
                                [natb_b[i2], const_b], [pab])
                pb_, pbb = self.bank()
                vb = bfv(pb_)
                for pg in range(4):
                    self.tr(vb[0:ROPE, pg * P:(pg + 1) * P], nk_[:, pg, :], ident_b[:, :], [natb_b[i2], const_b], [pbb])
                self.copy("dve", ckvT_sb[st_][:, :], va[:, 0:1024], [pab], [sb_["ckvT_sb"]])
                self.copy("act", krT_sb[st_][0:ROPE, :], vb[0:ROPE, 0:512], [pbb], [sb_["krT_sb"]])

            def st_b(i):
                b, s = divmod(i, NSMP)
                st_ = i % NSET
                sb_ = sblk_b[st_]
                col = NMETA + s
                pss, pssb = self.bank()
                self.mm(pss[0:8, 0:512], qlatT[:, 0, :, col], ckvT_sb[st_][:, 0:512], True, False, [smp_b, sb_["ckvT_sb"]], [pssb])
                self.mm(pss[0:8, 0:512], qlatT[:, 1, :, col], ckvT_sb[st_][:, 512:1024], False, False, [smp_b, sb_["ckvT_sb"]], [pssb])
                self.mm(pss[0:8, 0:512], qropeT[:, :, col], krT_sb[st_][0:ROPE, :], False, True, [smp_b, sb_["krT_sb"]], [pssb])
                mcol = 24 + 2 * s
                if b == 0:
                    self.S.add("dve", lambda e, pss=pss, mcol=mcol: e.reduce_max(out=stat2[0:8, mcol:mcol + 1], in_=pss[0:8, 0:512], axis=AX.X),
                               [pssb], [stat2_b])
                    self.ts("dve", stat2[0:8, mcol + 1:mcol + 2], stat2[0:8, mcol:mcol + 1], -MLA_SCALE, None, ALU.mult, None, [stat2_b], [stat2_b])
                self.activation(p_sb[st_][0:8, :], pss[0:8, 0:512], AF.Exp, [pssb, stat2_b], [sb_["p_sb"], lsum_b[s]],
                                bias=stat2[0:8, mcol + 1:mcol + 2], scale=MLA_SCALE,
                                accum_out=lsum[0:8, s * NLS + b:s * NLS + b + 1])

            def st_c(i):
                st_ = i % NSET
                sb_ = sblk_b[st_]
                pp, ppb = self.bank()
                vp = bfv(pp)
                for pg in range(4):
                    self.tr(vp[:, pg * 8:(pg + 1) * 8], p_sb[st_][0:8, pg * P:(pg + 1) * P], ident_b[0:8, 0:8], [sb_["p_sb"], const_b], [ppb])
                self.copy("dve", pT_sb[st_][:, 0:32], vp[:, 0:32], [ppb], [sb_["pT_sb"]])

            def st_d(i):
                b, s = divmod(i, NSMP)
                i2 = i % NNAT
                st_ = i % NSET
                sb_ = sblk_b[st_]
                acc, accb = accs[s]
                nc_ = natc[i2]
                for pg in range(4):
                    self.mm(acc[0:8, 0:256], pT_sb[st_][:, pg * 8:(pg + 1) * 8], nc_[:, pg, :], b == 0 and pg == 0, False,
                            [sb_["pT_sb"], natb_b[i2]], [accb])

            PF = NNAT - 1
            for i in range(min(PF, NB)):
                st_g(i)
            st_a(0)
            for i in range(NB):
                if i + 1 < NB:
                    st_a(i + 1)
                st_b(i)
                if i >= 1:
                    st_d(i - 1)
                st_c(i)
                if i + PF < NB:
                    st_g(i + PF)
                yield
            st_d(NB - 1)
            for s in range(NSMP):
                acc, accb = accs[s]
                col = NMETA + s
                mcol = 24 + 2 * s
                pss, pssb = self.bank()
                self.mm(pss[0:8, 0:NSIDE], qlatT[:, 0, :, col], ckvTs[:, 0, :], True, False, [smp_b, side_b], [pssb])
                self.mm(pss[0:8, 0:NSIDE], qlatT[:, 1, :, col], ckvTs[:, 1, :], False, False, [smp_b, side_b], [pssb])
                self.mm(pss[0:8, 0:NSIDE], qT[64:96, :, col], krTs[64:96, 0:NSIDE], False, True, [qT_b, side_b], [pssb])
                self.activation(p20[0:8, :], pss[0:8, 0:NSIDE], AF.Exp, [pssb, stat2_b], [smp_b], bias=stat2[0:8, mcol + 1:mcol + 2], scale=MLA_SCALE)
                self.tt("dve", p20b[0:8, :], p20[0:8, :], smask[0:8, s * NSIDE:(s + 1) * NSIDE], ALU.mult, [smp_b, const_b], [smp_b])
                pp, ppb = self.bank()
                vp = bfv(pp)
                self.tr(vp[0:NSIDE, 0:8], p20b[0:8, 0:NSIDE], ident_b[0:8, 0:8], [smp_b, const_b], [ppb])
                self.copy("dve", pT20[0:NSIDE, 0:8], vp[0:NSIDE, 0:8], [ppb], [smp_b])
                self.mm(acc[0:8, 0:256], pT20[0:NSIDE, 0:8], ckvs_bf[0:NSIDE, 2:258], False, True, [smp_b, side_b], [accb])
                self.S.add("dve", lambda e, s=s: e.reduce_sum(out=lsum[0:8, s * NLS + nblk:s * NLS + nblk + 1], in_=p20b[0:8, :], axis=AX.X),
                           [smp_b], [lsum_b[s]])
                self.S.add("dve", lambda e, s=s: e.reduce_sum(out=stat2[0:8, s:s + 1], in_=lsum[0:8, s * NLS:(s + 1) * NLS], axis=AX.X),
                           [lsum_b[s]], [stat2_b])
                self.recip(stat2[0:8, s:s + 1], stat2[0:8, s:s + 1], [stat2_b], [stat2_b])
                self.ts("dve", ol_sb[0:8, :], acc[0:8, 0:256], stat2[0:8, s:s + 1], None, ALU.mult, None, [accb, stat2_b], [smp_b])
                pt, ptb = self.bank()
                v = bfv(pt)
                for rk in range(2):
                    self.tr(v[:, rk * 8:(rk + 1) * 8], ol_sb[0:8, rk * P:(rk + 1) * P], ident_b[0:8, 0:8], [smp_b, const_b], [ptb])
                self.copy("dve", olT[:, :, s * 8:(s + 1) * 8], v[:, 0:16].rearrange("p (k j) -> p k j", k=2), [ptb], [smp_b])
            ps, psb = self.bank()
            for rk in range(2):
                self.mm(ps[0:32, 0:512], olT[:, rk, :], w_uv_sb[:, rk, :], rk == 0, rk == 1, [smp_b, wres_b], [psb])
            self.tt("dve", am[0:32, :], ps[0:32, 0:512], bmask[:, :], ALU.mult, [psb, const_b], [smp_b])
            ps2, ps2b = self.bank()
            self.mm(ps2[0:NSIDE, 0:512], sel2[:, :], am[0:32, :], True, True, [const_b, smp_b], [ps2b])
            self.copy("act", a_tm[0:NSIDE, 0, :], ps2[0:NSIDE, 0:512], [ps2b], [a_tm_b])
            self.memset("pool", stat[0:1, 30:31], 0.0,
                        natb_b + [b_ for d_ in sblk_b for b_ in d_.values()] + [smp_b], [zq_b, zv_b, zg_b, stat_b])

        def mixer_out(pas):
            n = pas.n
            for gdst, goff, gb_ in ((ga_s, OFF_GA, zq_b), (gb_s, OFF_GB, zv_b)):
                for half in range(2):
                    def ev_g(t, ps, psb, half=half, gdst=gdst, gb_=gb_):
                        self.activation(gdst[t.slot][0:t.rows, half * 512:(half + 1) * 512], ps[0:t.rows, 0:512], AF.Sigmoid, [psb], [gb_])
                    proj_tm(pas, uT, uT_b, KD, w_in, goff + half * 512, 512, ev_g, allow_hi=True)
            for half in range(2):
                def ev_a(t, ps, psb, half=half):
                    self.tt("dve", m_tm[t.slot][0:t.rows, half * 512:(half + 1) * 512], ps[0:t.rows, 0:512],
                            ga_s[t.slot][0:t.rows, half * 512:(half + 1) * 512], ALU.mult, [psb, zq_b], [zg_b])
                proj_tm(pas, aT, aT_b, 4, w_mla_o, half * 512, 512, ev_a, allow_hi=True)
            for half in range(2):
                def ev_r(t, ps, psb, half=half):
                    r = t.rows
                    self.tt("dve", ma[0:r, :], ps[0:r, 0:512], gb_s[t.slot][0:r, half * 512:(half + 1) * 512], ALU.mult, [psb, zv_b], [ma_b])
                    self.tt("pool", m_tm[t.slot][0:r, half * 512:(half + 1) * 512], ma[0:r, :],
                            m_tm[t.slot][0:r, half * 512:(half + 1) * 512], ALU.add, [ma_b, zg_b], [zg_b])
                proj_tm(pas, rinT, rinT_b, KD, w_ret_o, half * 512, 512, ev_r, allow_hi=True)
            for t in pas.tiles:
                r = t.rows
                pt, ptb = self.bank()
                v = bfv(pt)
                for k8 in range(KD):
                    self.tr(v[:, k8 * P:k8 * P + r], m_tm[t.slot][0:r, k8 * P:(k8 + 1) * P], ident_b[0:r, 0:r], [zg_b, const_b], [ptb])
                evac_copy(mT[:, :, t.c0:t.c0 + r], v[:, 0:1024].rearrange("p (k c) -> p k c", k=KD)[:, :, 0:r], [ptb], [mT_b])
            for ig in range(4):
                w, wbuf = self.load_w(w_out[:, ig * 256:(ig + 1) * 256].rearrange("(k p) c -> p k c", p=P), KD, 256)
                for ii in range(2):
                    i = ig * 2 + ii
                    ps, psb = self.bank()
                    for k in range(KD):
                        self.mm(ps[:, 0:n], w[:, k, ii * P:(ii + 1) * P], mT[:, k, 0:n], k == 0, k == KD - 1, [wbuf, mT_b], [psb])
                    self.tt("dve", hT[:, i, 0:n], ps[:, 0:n], hT[:, i, 0:n], ALU.add, [psb, hT_b], [hT_b])

        self.reg_i = 0
        passes = [mk_pass("side", 0)] + [mk_pass("main", c) for c in range(nch)]
        try:
            load_x_dma(passes[0])
            for pi_, pas in enumerate(passes):
                load_x(pas)
                chk(pas.kind + ":load")
                ffn(pas.n, 0, W["g1"], W["u1"], W["d1"])
                chk(pas.kind + ":ffn1")
                mixer_proj(pas)
                chk(pas.kind + ":proj")
                if pas.kind == "side":
                    gs_ = mla_samples(pas)
                    gr_ = retention_side(pas)
                    nb_tot = NSMP * (npages // 4)
                    every = max(1, nb_tot // 24)
                    ib = 0
                    done_r = False
                    for _ in gs_:
                        ib += 1
                        if not done_r and ib % every == 0:
                            try:
                                next(gr_)
                            except StopIteration:
                                done_r = True
                    for _ in gr_:
                        pass
                else:
                    ga_ = mla_main(pas)
                    gr_ = retention_main(pas)
                    n_att = HM * (3 + 4 * (pas.c + 1) + 1)
                    n_ret = 4 * 5
                    done_a = done_r = False
                    ia = ir = 0
                    while not (done_a and done_r):
                        if not done_a and (done_r or ia * n_ret <= ir * n_att):
                            try:
                                next(ga_)
                                ia += 1
                            except StopIteration:
                                done_a = True
                        elif not done_r:
                            try:
                                next(gr_)
                                ir += 1
                            except StopIteration:
                                done_r = True
                a_transpose(pas)
                mixer_out(pas)
                chk(pas.kind + ":mixout")
                ffn(pas.n, 2, W["g2"], W["u2"], W["d2"])
                if pi_ + 1 < len(passes):
                    load_x_dma(passes[pi_ + 1])
                final_norm_out(pas)
                chk(pas.kind + ":end")
        except _Stop:
            pass
        for h in range(RH):
            self.dma(S_p[h], S[:, h, :], [S_b], [], "o_Sp")

        self.S.finalize()
        with contextlib.ExitStack() as es2:
            sems = {e: es2.enter_context(nc.semaphore(f"s_{e}")) for e in COMPUTE}
            dsems = {n_: es2.enter_context(nc.semaphore(f"d_{n_}")) for n_ in self.dma_sem_names}
            finals = [(n_, v_) for n_, v_ in self.S.dma_counts.items() if n_.startswith("o_")]
            self.sp_regs_ctx = es2
            self.S.emit(nc, sems, dsems, finals)
        self.es.close()
        return nc

    def _alloc_regs(self, e):
        self.sp_regs.clear()
        for i in range(4):
            self.sp_regs.append(self.sp_regs_ctx.enter_context(e.register(f"pg{i}")))


def host_consts(nch=4, npages=NPAGES):
    NTT = 1 + nch * 4
    f32 = np.float32
    c = {}
    c["c_ident"] = np.eye(P, dtype=f32)
    c["c_iota"] = (np.arange(P) % 32).astype(f32)[:, None].copy()
    pos = np.zeros((NTT, P), dtype=np.int64)
    pos[0, :NMETA] = np.arange(NMETA)
    pos[0, NMETA:NSIDE] = npages * PAGE
    for gt in range(1, NTT):
        pos[gt] = NMETA + (gt - 1) * P + np.arange(P)
    for nm, half in (("R", 64), ("M", 16)):
        inv = (f32(10000.0) ** (-(np.arange(half, dtype=f32)) / f32(half))).astype(f32)
        ang = (pos.astype(f32)[:, :, None] * inv[None, None, :]).astype(f32)
        c["c_cos" + nm] = np.cos(ang).astype(f32)
        c["c_sin" + nm] = np.sin(ang).astype(f32)
    lg = np.log1p(-np.exp2(-5.0 - np.arange(RH))).astype(np.float64)
    idx = np.arange(P, dtype=np.float64)
    diff = idx[None, :] - idx[:, None]
    maskT = np.zeros((P, RH, P), dtype=f32)
    qdec = np.zeros((P, RH, P), dtype=f32)
    kdec = np.zeros((P, 8), dtype=f32)
    for h in range(RH):
        maskT[:, h, :] = np.where(diff >= 0, np.exp(lg[h] * np.maximum(diff, 0.0)), 0.0)
        qdec[:, h, :] = np.exp(lg[h] * (idx + 1.0))[None, :]
        kdec[:, h] = np.exp(lg[h] * (P - 1.0 - idx))
        kdec[:NMETA, 4 + h] = np.exp(lg[h] * (NMETA - 1.0 - idx[:NMETA]))
    c["c_maskT"], c["c_qdec"], c["c_kdec"] = maskT, qdec, kdec
    c["c_tri"] = (idx[:, None] <= idx[None, :]).astype(f32)
    oh = np.zeros((P, NSMP), dtype=f32)
    bm = np.zeros((32, 512), dtype=f32)
    sel2 = np.zeros((32, NSIDE), dtype=f32)
    sm = np.zeros((8, NSMP * NSIDE), dtype=f32)
    for s in range(NSMP):
        oh[NMETA + s, s] = 1.0
        sm[:, s * NSIDE + NMETA + s] = 1.0
        for h in range(HM):
            bm[s * 8 + h, h * VD:(h + 1) * VD] = 1.0
            sel2[s * 8 + h, NMETA + s] = 1.0
    c["c_onehot"], c["c_bmask"], c["c_sel2"], c["c_smask"] = oh, bm, sel2, sm
    return c


_WNAMES = ["ffn1_norm", "ffn1_gate", "ffn1_up", "ffn1_down", "mix_norm", "w_in", "q_norm", "kv_norm", "w_uq",
           "w_mla_o", "ret_gn", "w_ret_o", "w_out", "ffn2_norm", "ffn2_gate", "ffn2_up", "ffn2_down"]


def make_in_maps(inputs, ncores, nch, npages):
    f32 = np.float32
    consts = host_consts(nch, npages)
    shared = {}
    for n_ in _WNAMES:
        shared[n_] = np.ascontiguousarray(np.asarray(inputs[n_], dtype=f32)[0])
    shared["w_uk"] = np.ascontiguousarray(np.asarray(inputs["w_uk"], dtype=f32)[0].reshape(KVR, HM * NOPE))
    shared["w_uv"] = np.ascontiguousarray(np.asarray(inputs["w_uv"], dtype=f32)[0].reshape(KVR, HM * VD))
    shared["final_norm"] = np.ascontiguousarray(np.asarray(inputs["final_norm"], dtype=f32))
    shared["meta"] = np.ascontiguousarray(np.asarray(inputs["meta_tokens"], dtype=f32))
    shared["cache_ckv"] = np.ascontiguousarray(np.asarray(inputs["cache_ckv"], dtype=f32)[0])
    shared["cache_krope"] = np.ascontiguousarray(np.asarray(inputs["cache_krope"], dtype=f32)[0])
    shared.update(consts)
    xp = np.asarray(inputs["x_prompt"], dtype=f32)
    xsm = np.asarray(inputs["x_sample"], dtype=f32)
    st = np.asarray(inputs["state_ret"], dtype=f32)[0]
    pt = np.asarray(inputs["page_table"], dtype=np.int32)
    maps = []
    for c in range(ncores):
        m = dict(shared)
        m["x"] = np.ascontiguousarray(xp[c])
        m["xs"] = np.ascontiguousarray(xsm[c * NSMP:(c + 1) * NSMP, 0, :])
        m["state"] = np.ascontiguousarray(st[c * NSMP:(c + 1) * NSMP])
        m["ptab"] = np.ascontiguousarray(pt[c * NSMP:(c + 1) * NSMP])
        maps.append(m)
    return maps


def gather_outputs(results, ncores):
    f32 = np.float32
    y = np.stack([r["y"] for r in results]).astype(f32)
    ys = np.concatenate([r["ys"] for r in results])[:, None, :].astype(f32)
    ckv_p = np.stack([r["ckv_p"] for r in results])[None].astype(f32)
    kr_p = np.stack([r["kr_p"] for r in results])[None].astype(f32)
    S_p = np.stack([r["S_p"] for r in results])[None].astype(f32)
    ckv_s = np.concatenate([r["ckv_s"] for r in results])[None, :, None, :].astype(f32)
    kr_s = np.concatenate([r["kr_s"] for r in results])[None, :, None, :].astype(f32)
    S_s = np.concatenate([r["S_s"] for r in results])[None].astype(f32)
    return (y, ys, ckv_p, kr_p, S_p, ckv_s, kr_s, S_s)


def kernel(**inputs):
    nch = SEQ // CH
    b = Builder(dict(nch=nch, npages=NPAGES))
    nc = b.build()
    maps = make_in_maps(inputs, NCORES, nch, NPAGES)
    res = run_bass_kernel_spmd(nc, maps, core_ids=list(range(NCORES)))
    return gather_outputs(res.results, NCORES)
```
